# Optimizing a Trainium2 kernel written in Bass

```python
import math
import jax
import jax.numpy as jnp
from jax import lax
import numpy as np

D_MODEL = 1024
BATCH = 4
SEQ = 8192
DEPTH = 1
DEC_BATCH = 16
DEC_SEQ = 2048
PAST_LEN = 128

D_MIX = D_MODEL
RW_HEADS = 8
RW_N = 64
RW_C = RW_HEADS * RW_N
LORA_W = 64
LORA_A = 64
LORA_G = 128
DECAY_SCALE = 0.606531
GN_EPS = 64e-5
DA_HEADS = 4
DA_DH = 64
DA_C = DA_HEADS * 2 * DA_DH
ROT_DIM = DA_DH // 4
ROPE_THETA = 500000.0
Q_BLOCK = 128
CONV_W = 3
D_FF = 2816
NORM_EPS = 1e-6
RW_COLS = 3 * RW_C + 2 * LORA_W + 2 * LORA_A + LORA_G
IN_COLS = RW_COLS + 3 * DA_C

kernel_name = 'hybrid_rwkv7_diffattn_macaron_encoder'


def rms_norm(x, g, eps=NORM_EPS):
    xf = x.astype(jnp.float32)
    y = xf * lax.rsqrt(jnp.mean(xf * xf, axis=-1, keepdims=True) + eps)
    return (y * g.astype(jnp.float32)).astype(x.dtype)


def swiglu_ffn(x, norm_g, w_gu, w_down):
    h = rms_norm(x, norm_g)
    gate, up = jnp.split(h @ w_gu, 2, axis=-1)
    return (jax.nn.silu(gate) * up) @ w_down


def centred_short_conv(u, taps):
    prev = jnp.pad(u[:, :-1], ((0, 0), (1, 0), (0, 0)))
    nxt = jnp.pad(u[:, 1:], ((0, 0), (0, 1), (0, 0)))
    return prev * taps[0] + u * taps[1] + nxt * taps[2]


def rope_tables(seq_len):
    inv_freq = ROPE_THETA ** (-jnp.arange(0, ROT_DIM, 2, dtype=jnp.float32) / ROT_DIM)
    ang = jnp.arange(seq_len, dtype=jnp.float32)[:, None] * inv_freq[None, :]
    return jnp.cos(ang), jnp.sin(ang)


def partial_rope(x, cos, sin):
    c = cos[None, :, None, None, :]
    s = sin[None, :, None, None, :]
    half = ROT_DIM // 2
    x1 = x[..., :half]
    x2 = x[..., half:ROT_DIM]
    return jnp.concatenate([x1 * c - x2 * s, x2 * c + x1 * s, x[..., ROT_DIM:]], axis=-1)


def wkv7_scan(r, w, k, v, kk, a, reverse):
    B, _, H, N = r.shape
    xs = tuple(jnp.moveaxis(t, 1, 0) for t in (r, w, k, v, kk, a))

    def step(state, inp):
        r_t, w_t, k_t, v_t, kk_t, a_t = inp
        sa = jnp.einsum('bhij,bhj->bhi', state, -kk_t)
        state = (state * w_t[:, :, None, :]
                 + sa[..., None] * (kk_t * a_t)[:, :, None, :]
                 + v_t[..., None] * k_t[:, :, None, :])
        y_t = jnp.einsum('bhij,bhj->bhi', state, r_t)
        return state, y_t

    s0 = jnp.zeros((B, H, N, N), jnp.float32)
    _, ys = lax.scan(step, s0, xs, reverse=reverse)
    return jnp.moveaxis(ys, 0, 1)


def rwkv7_bidir_mixer(u, w0, w_up, a0, a_up, g_up, k_k, k_a, r_k, ln_g, ln_b):
    B, S, _ = u.shape
    uf = u.astype(jnp.float32)
    o_w = 3 * RW_C
    o_a = o_w + 2 * LORA_W
    o_g = o_a + 2 * LORA_A
    r = uf[..., 0:RW_C]
    k = uf[..., RW_C:2 * RW_C]
    v = uf[..., 2 * RW_C:3 * RW_C]
    wd = uf[..., o_w:o_a].reshape(B, S, 2, LORA_W)
    ad = uf[..., o_a:o_g].reshape(B, S, 2, LORA_A)
    gd = uf[..., o_g:RW_COLS]
    w = jnp.exp(-DECAY_SCALE * jax.nn.sigmoid(w0 + jnp.einsum('bsdr,drc->bsdc', jnp.tanh(wd), w_up)))
    a = jax.nn.sigmoid(a0 + jnp.einsum('bsdr,drc->bsdc', ad, a_up))
    g = jax.nn.sigmoid(gd) @ g_up

    def heads(t):
        return t.reshape(t.shape[:-1] + (RW_HEADS, RW_N))

    kk = heads(k * k_k)
    kk = kk / jnp.maximum(jnp.sqrt(jnp.sum(kk * kk, axis=-1, keepdims=True)), 1e-12)
    k_dir = k[:, :, None, :] * (1.0 + (a - 1.0) * k_a)
    rh = heads(r)
    vh = heads(v)
    kh = heads(k_dir)
    wh = heads(w)
    ah = heads(a)
    y = (wkv7_scan(rh, wh[:, :, 0], kh[:, :, 0], vh, kk, ah[:, :, 0], False)
         + wkv7_scan(rh, wh[:, :, 1], kh[:, :, 1], vh, kk, ah[:, :, 1], True))
    mu = jnp.mean(y, axis=-1, keepdims=True)
    var = jnp.mean(jnp.square(y - mu), axis=-1, keepdims=True)
    y = ((y - mu) * lax.rsqrt(var + GN_EPS)).reshape(B, S, RW_C) * ln_g + ln_b
    coef = jnp.sum(jnp.sum(rh[:, :, None] * kh * r_k, axis=-1, keepdims=True), axis=2)
    bonus = (coef * vh).reshape(B, S, RW_C)
    return ((y + bonus) * g).astype(u.dtype)


def diff_attention(q, k, v, q_norm, k_norm, lq1, lk1, lq2, lk2, subln, cos, sin, lam_init):
    B, S, _ = q.shape
    f32 = jnp.float32
    q = q.astype(f32).reshape(B, S, DA_HEADS, 2, DA_DH)
    k = k.astype(f32).reshape(B, S, DA_HEADS, 2, DA_DH)
    v = v.astype(f32).reshape(B, S, DA_HEADS, 2 * DA_DH)
    q = partial_rope(rms_norm(q, q_norm), cos, sin) * (DA_DH ** -0.5)
    k = partial_rope(rms_norm(k, k_norm), cos, sin)
    lam = (jnp.exp(jnp.sum(lq1.astype(f32) * lk1.astype(f32)))
           - jnp.exp(jnp.sum(lq2.astype(f32) * lk2.astype(f32)))) + lam_init
    n_blk = S // Q_BLOCK
    q_blocks = jnp.moveaxis(q.reshape(B, n_blk, Q_BLOCK, DA_HEADS, 2, DA_DH), 1, 0)

    def attend(q_blk):
        s = jnp.einsum('bqhcd,bkhcd->bhcqk', q_blk, k)
        p = jax.nn.softmax(s, axis=-1)
        diff = p[:, :, 0] - lam * p[:, :, 1]
        return jnp.einsum('bhqk,bkhe->bqhe', diff, v)

    o = lax.map(attend, q_blocks)
    o = jnp.moveaxis(o, 0, 1).reshape(B, S, DA_HEADS, 2 * DA_DH)
    o = rms_norm(o, subln) * (1.0 - lam_init)
    return o.reshape(B, S, DA_C)


def encoder_layer(x, cos, sin, lam_init,
                  ffn1_norm, ffn1_w_gu, ffn1_w_down,
                  mix_norm, w_in, conv_w,
                  rw_w0, rw_w_up, rw_a0, rw_a_up, rw_g_up, rw_k_k, rw_k_a, rw_r_k, rw_ln_g, rw_ln_b,
                  da_q_norm, da_k_norm, da_lq1, da_lk1, da_lq2, da_lk2, da_subln,
                  w_out, ffn2_norm, ffn2_w_gu, ffn2_w_down, final_norm):
    x = x + 0.5 * swiglu_ffn(x, ffn1_norm, ffn1_w_gu, ffn1_w_down)
    h = rms_norm(x, mix_norm)
    proj = h @ w_in
    rw_in = centred_short_conv(proj[..., :RW_COLS], conv_w)
    da_q, da_k, da_v = jnp.split(proj[..., RW_COLS:], 3, axis=-1)
    y_rw = rwkv7_bidir_mixer(rw_in, rw_w0, rw_w_up, rw_a0, rw_a_up, rw_g_up,
                             rw_k_k, rw_k_a, rw_r_k, rw_ln_g, rw_ln_b)
    y_da = diff_attention(da_q, da_k, da_v, da_q_norm, da_k_norm, da_lq1, da_lk1, da_lq2, da_lk2,
                          da_subln, cos, sin, lam_init)
    mixed = jnp.concatenate([y_rw.astype(x.dtype), y_da.astype(x.dtype)], axis=-1) @ w_out
    x = x + mixed.astype(x.dtype)
    x = x + 0.5 * swiglu_ffn(x, ffn2_norm, ffn2_w_gu, ffn2_w_down)
    return rms_norm(x, final_norm)


def setup_inputs(seed: int = 0) -> dict:
    key = jax.random.key(seed)
    ks = jax.random.split(key, 36)
    f32 = jnp.float32
    L = DEPTH

    def nrm(k, shape, scale):
        return jax.random.normal(k, shape, f32) * scale

    return {
        'x_prompt': nrm(ks[0], (BATCH, SEQ, D_MODEL), 1.0),
        'x_sample': nrm(ks[1], (DEC_BATCH, DEC_SEQ, D_MODEL), 1.0),
        'ffn1_norm': 1.0 + nrm(ks[2], (L, D_MODEL), 0.02),
        'ffn1_w_gu': nrm(ks[3], (L, D_MODEL, 2 * D_FF), D_MODEL ** -0.5),
        'ffn1_w_down': nrm(ks[4], (L, D_FF, D_MODEL), D_FF ** -0.5),
        'mix_norm': 1.0 + nrm(ks[5], (L, D_MODEL), 0.02),
        'w_in': nrm(ks[6], (L, D_MODEL, IN_COLS), D_MODEL ** -0.5),
        'conv_w': jnp.array([0.25, 0.5, 0.25], f32)[None, :, None] + nrm(ks[7], (L, CONV_W, RW_COLS), 0.05),
        'rw_w0': nrm(ks[8], (L, 2, RW_C), 0.5),
        'rw_w_up': nrm(ks[9], (L, 2, LORA_W, RW_C), 0.1 * LORA_W ** -0.5),
        'rw_a0': nrm(ks[10], (L, 2, RW_C), 0.5),
        'rw_a_up': nrm(ks[11], (L, 2, LORA_A, RW_C), 0.1 * LORA_A ** -0.5),
        'rw_g_up': nrm(ks[12], (L, LORA_G, RW_C), LORA_G ** -0.5),
        'rw_k_k': 0.85 + nrm(ks[13], (L, RW_C), 0.02),
        'rw_k_a': 1.0 + nrm(ks[14], (L, RW_C), 0.02),
        'rw_r_k': nrm(ks[15], (L, RW_HEADS, RW_N), 0.1),
        'rw_ln_g': 1.0 + nrm(ks[16], (L, RW_C), 0.02),
        'rw_ln_b': nrm(ks[17], (L, RW_C), 0.01),
        'da_q_norm': 1.0 + nrm(ks[18], (L, DA_DH), 0.02),
        'da_k_norm': 1.0 + nrm(ks[19], (L, DA_DH), 0.02),
        'da_lq1': nrm(ks[20], (L, DA_DH), 0.1),
        'da_lk1': nrm(ks[21], (L, DA_DH), 0.1),
        'da_lq2': nrm(ks[22], (L, DA_DH), 0.1),
        'da_lk2': nrm(ks[23], (L, DA_DH), 0.1),
        'da_subln': 1.0 + nrm(ks[24], (L, 2 * DA_DH), 0.02),
        'w_out': nrm(ks[25], (L, D_MIX, D_MODEL), D_MIX ** -0.5),
        'ffn2_norm': 1.0 + nrm(ks[26], (L, D_MODEL), 0.02),
        'ffn2_w_gu': nrm(ks[27], (L, D_MODEL, 2 * D_FF), D_MODEL ** -0.5),
        'ffn2_w_down': nrm(ks[28], (L, D_FF, D_MODEL), D_FF ** -0.5),
        'final_norm': 1.0 + nrm(ks[29], (L, D_MODEL), 0.02),
    }


def reference(x_prompt, x_sample, ffn1_norm, ffn1_w_gu, ffn1_w_down, mix_norm, w_in, conv_w,
              rw_w0, rw_w_up, rw_a0, rw_a_up, rw_g_up, rw_k_k, rw_k_a, rw_r_k, rw_ln_g, rw_ln_b,
              da_q_norm, da_k_norm, da_lq1, da_lk1, da_lq2, da_lk2, da_subln,
              w_out, ffn2_norm, ffn2_w_gu, ffn2_w_down, final_norm):
    cos_p, sin_p = rope_tables(x_prompt.shape[1])
    cos_s, sin_s = rope_tables(x_sample.shape[1])
    y_prompt = x_prompt
    y_sample = x_sample
    for l in range(DEPTH):
        lam_init = 0.8 - 0.6 * math.exp(-0.3 * l)
        params = (ffn1_norm[l], ffn1_w_gu[l], ffn1_w_down[l], mix_norm[l], w_in[l], conv_w[l],
                  rw_w0[l], rw_w_up[l], rw_a0[l], rw_a_up[l], rw_g_up[l], rw_k_k[l], rw_k_a[l],
                  rw_r_k[l], rw_ln_g[l], rw_ln_b[l],
                  da_q_norm[l], da_k_norm[l], da_lq1[l], da_lk1[l], da_lq2[l], da_lk2[l], da_subln[l],
                  w_out[l], ffn2_norm[l], ffn2_w_gu[l], ffn2_w_down[l], final_norm[l])
        y_prompt = encoder_layer(y_prompt, cos_p, sin_p, lam_init, *params)
        y_sample = encoder_layer(y_sample, cos_s, sin_s, lam_init, *params)
    return (y_prompt, y_sample)
```

```python
import contextlib
import numpy as np
import concourse.bass as bass
import concourse.mybir as mybir
from concourse.bass_utils import run_bass_kernel_spmd

F32 = mybir.dt.float32
BF16 = mybir.dt.bfloat16
AF = mybir.ActivationFunctionType
ALU = mybir.AluOpType
AX = mybir.AxisListType

ENGS = ['pe', 'act', 'dve', 'pool', 'sp']
D = 1024
DFF = 2816
NJ = 22
KC = 8
DECAY = 0.606531
GN_EPS = 64e-5
NEG = 30000.0


def I(name, *a, **kw):
    return (name, a, kw)


class Buf:
    __slots__ = ('w', 'rs', 'dsem')

    def __init__(self):
        self.w = None
        self.rs = []
        self.dsem = None


DEBUGLOG = None


class Sched:
    def __init__(self, nc):
        self.nc = nc
        self.ops = {e: [] for e in ENGS}
        self.known = {e: {} for e in ENGS}
        self.ndsem = 0
        self.dcount = {}
        self.stack = contextlib.ExitStack()
        self.ntens = 0
        self.pending = {}

    def sb(self, shape, dtype, name=None):
        self.ntens += 1
        return self.stack.enter_context(self.nc.sbuf_tensor(f"t{self.ntens}", list(shape), dtype))

    def ps(self, shape, dtype):
        self.ntens += 1
        return self.stack.enter_context(self.nc.psum_tensor(f"p{self.ntens}", list(shape), dtype))

    def _deps(self, reads, writes):
        deps = []
        for b in reads:
            if b.w is not None:
                deps.append(b.w)
        for b in writes:
            if b.w is not None:
                deps.append(b.w)
            deps.extend(b.rs)
        return deps

    def _filter(self, eng, deps, force_same=False):
        kn = self.known[eng]
        best = {}
        for t in deps:
            if t[0] == 'e' and t[1] == eng and eng == 'pe' and not force_same:
                continue
            k = (t[0], t[1])
            if k not in best or best[k][2] < t[2]:
                best[k] = t
        waits = []
        for k, t in best.items():
            if t[0] == 'e':
                if kn.get(t[1], -1) >= t[2]:
                    continue
                kn[t[1]] = t[2]
                self.ops[t[1]][t[2]]['inc'] = True
            else:
                if kn.get(k, 0) >= t[2]:
                    continue
                kn[k] = t[2]
            waits.append(t)
        return waits

    def op(self, eng, fn, reads=(), writes=(), extra=(), force_same=False):
        deps = self._deps(reads, writes) + list(extra) + self.pending.pop(eng, [])
        waits = self._filter(eng, deps, force_same)
        idx = len(self.ops[eng])
        self.ops[eng].append(dict(fn=fn, waits=waits, inc=False, dma=None))
        tok = ('e', eng, idx)
        for b in reads:
            b.rs.append(tok)
            if len(b.rs) > 64:
                b.rs = self._compact(b.rs)
        for b in writes:
            b.w = tok
            b.rs = []
        return tok

    @staticmethod
    def _compact(rs):
        best = {}
        for t in rs:
            k = (t[0], t[1])
            if k not in best or best[k][2] < t[2]:
                best[k] = t
        return list(best.values())

    def dma(self, eng, fns, sembuf, reads=(), writes=()):
        if isinstance(fns, tuple):
            fns = [fns]
        if sembuf.dsem is None:
            sembuf.dsem = self.ndsem
            self.dcount[self.ndsem] = 0
            self.ndsem += 1
        sem = sembuf.dsem
        deps = self._deps(reads, writes) + self.pending.pop(eng, [])
        if self.dcount[sem] > 0:
            deps.append(('d', sem, self.dcount[sem]))
        waits = self._filter(eng, deps)
        self.dcount[sem] += 16 * len(fns)
        tok = ('d', sem, self.dcount[sem])
        self.ops[eng].append(dict(fn=fns, waits=waits, inc=False, dma=sem))
        for b in reads:
            b.rs.append(tok)
        for b in writes:
            b.w = tok
            b.rs = []
        return tok

    def barrier(self):
        toks = []
        for e in ENGS:
            n = len(self.ops[e])
            while n > 0 and self.ops[e][n - 1]['dma'] is not None:
                n -= 1
            if n > 0:
                toks.append(('e', e, n - 1))
        for sem, cnt in self.dcount.items():
            if cnt > 0:
                toks.append(('d', sem, cnt))
        for e in ENGS:
            self.pending[e] = list(toks)

    def emit(self):
        nc = self.nc
        fin = []
        for sem, cnt in self.dcount.items():
            if cnt > 0:
                fin.append(('d', sem, cnt))
        self.ops['sp'].append(dict(fn=None, waits=fin, inc=False, dma=None))
        vals = {}
        for e in ENGS:
            c = 0
            v = []
            for o in self.ops[e]:
                if o['inc']:
                    c += 1
                v.append(c)
            vals[e] = v
        st = self.stack
        esem = {e: st.enter_context(nc.semaphore(f"es_{e}")) for e in ENGS}
        dsem = [st.enter_context(nc.semaphore(f"ds_{i}")) for i in range(self.ndsem)]
        block = st.enter_context(nc.Block())

        def run(e, engobj):
            for o in self.ops[e]:
                for t in o['waits']:
                    if t[0] == 'e':
                        engobj.wait_ge(esem[t[1]], vals[t[1]][t[2]])
                    else:
                        engobj.wait_ge(dsem[t[1]], t[2])
                if o['fn'] is None:
                    continue
                if o['dma'] is not None:
                    for f in o['fn']:
                        _i = getattr(engobj, f[0])(*f[1], **f[2])
                        _i.then_inc(dsem[o['dma']], 16)
                        if DEBUGLOG is not None:
                            DEBUGLOG.append((e, _i.ins.name, str(f[2].get('out'))[:200], str(f[2].get('in_'))[:200]))
                else:
                    f = o['fn']
                    ins = getattr(engobj, f[0])(*f[1], **f[2])
                    if o['inc']:
                        ins.then_inc(esem[e], 1)

        @block.tensor
        def _(eng):
            run('pe', eng)

        @block.scalar
        def _(eng):
            run('act', eng)

        @block.vector
        def _(eng):
            run('dve', eng)

        @block.gpsimd
        def _(eng):
            run('pool', eng)

        @block.sync
        def _(eng):
            run('sp', eng)

    def close(self):
        self.stack.close()


class Arena:
    def __init__(self, S, words):
        self.t = S.sb([128, words], F32)
        self.words = words
        self.off = 0

    def reset(self, off=0):
        self.off = off

    def f32(self, n):
        a = self.t[:, self.off:self.off + n]
        self.off += n
        assert self.off <= self.words, f"arena overflow {self.off}"
        return a

    def bf16(self, n):
        w = (n + 1) // 2
        a = self.t[:, self.off:self.off + w].bitcast(BF16)
        self.off += w
        assert self.off <= self.words, f"arena overflow {self.off}"
        return a


class Rot:
    def __init__(self, items):
        self.items = items
        self.i = 0

    def next(self):
        it = self.items[self.i % len(self.items)]
        self.i += 1
        return it


def build(NT, SEG, T, dbg=False, stop=None):
    nc = bass.Bass("TRN2", target_bir_lowering=False)
    S = Sched(nc)
    NTL = NT // T
    SUB = T // 128
    NCH = T // 64
    NKT = NT // 128

    def din(name, shape):
        return nc.dram_tensor(name, list(shape), F32, kind="ExternalInput").ap()

    def dscr(name, shape, dt):
        if dbg:
            return nc.dram_tensor(name, list(shape), dt, kind="ExternalOutput").ap()
        return nc.dram_tensor(name, list(shape), dt).ap()

    def finish():
        S.emit()
        S.close()
        return nc

    x = din("x", [NT, D])
    wgu = [din("wgu1", [D, 2 * DFF]), din("wgu2", [D, 2 * DFF])]
    wdn = [din("wd1", [DFF, D]), din("wd2", [DFF, D])]
    win = din("win", [D, 3456])
    wout = din("wout", [D, D])
    gvec = din("gvec", [128, 4, 8])
    taps = din("taps", [128, 3, 15])
    w0c = din("w0c", [128, 2, 4])
    a0c = din("a0c", [128, 2, 4])
    wup = din("wup", [128, 512])
    aup = din("aup", [128, 512])
    gup = din("gup", [128, 512])
    rwv = din("rwv", [128, 5, 4])
    gqk = din("gqk", [128, 2])
    gqkb = din("gqkb", [128, 2, 64])
    lamb = din("lamb", [128, 4, 64])
    sublnb = din("sublnb", [128, 128])
    cident = din("cident", [128, 128])
    cblk = din("cblk", [128, 128])
    cperm = din("cperm", [128, 128])
    cmask = din("cmask", [128, 2, 128])
    cmaskt = din("cmaskt", [128, 2, 64])
    cid64 = din("cid64", [128, 64])
    cscan = din("cscan", [128, 4 * T])
    costab = din("costab", [128, NT])
    sintab = din("sintab", [128, NT])
    flagin = din("flag", [128, 1])
    y = nc.dram_tensor("y", [NT, D], F32, kind="ExternalOutput").ap()

    wgu_t = [dscr("wgu1t", [NJ, 128, 8 * 256], BF16), dscr("wgu2t", [NJ, 128, 8 * 256], BF16)]
    wdn_t = [dscr("wd1t", [KC, 128, NJ * 128], BF16), dscr("wd2t", [KC, 128, NJ * 128], BF16)]
    win_bf = dscr("winb", [D, 3456], BF16)
    wout_bf = dscr("woutb", [D, D], BF16)
    wconv = dscr("wconv", [15, 128, 3 * 8 * 128], BF16)
    X1T = dscr("x1t", [8, 128, NT], F32)
    URW = dscr("urw", [15, 128, NT], F32)
    QT = dscr("qt", [4, 128, NT], BF16)
    KT = dscr("kt", [4, 128, NT], BF16)
    VDA = dscr("vda", [NT, 512], BF16)
    YRW = dscr("yrw", [2, 4, 128, NT], F32)
    YMIX = dscr("ymix", [8, 128, NT], BF16)

    PSD = [S.ps([128, 1024], F32) for _ in range(4)]
    PSB = [(PSD[i // 2][:, (i % 2) * 512:(i % 2 + 1) * 512], Buf()) for i in range(8)]
    psrot = Rot(PSB)

    CW = 6200
    cst = S.sb([128, CW], F32)
    coff = [0]

    def cf32(n):
        a = cst[:, coff[0]:coff[0] + n]
        coff[0] += n
        assert coff[0] <= CW
        return a

    def cbf(n):
        w = (n + 1) // 2
        a = cst[:, coff[0]:coff[0] + w].bitcast(BF16)
        coff[0] += w
        assert coff[0] <= CW
        return a

    cb = Buf()

    def ld(dst, src, eng='sp'):
        S.dma(eng, I('dma_start', out=dst, in_=src), cb, writes=[cb])

    identF = cf32(128); ld(identF, cident)
    blkF = cf32(128); ld(blkF, cblk)
    permF = cf32(128); ld(permF, cperm)
    maskF = cf32(256); ld(maskF, cmask.rearrange("p a b -> p (a b)"))
    masktF = cf32(128); ld(masktF, cmaskt.rearrange("p a b -> p (a b)"))
    id64F = cf32(64); ld(id64F, cid64)
    gv = cf32(32); ld(gv, gvec.rearrange("p a b -> p (a b)"))
    tp = cf32(45); ld(tp, taps.rearrange("p a b -> p (a b)"))
    w0t = cf32(8); ld(w0t, w0c.rearrange("p a b -> p (a b)"))
    a0t = cf32(8); ld(a0t, a0c.rearrange("p a b -> p (a b)"))
    rwt = cf32(20); ld(rwt, rwv.rearrange("p a b -> p (a b)"))
    gqt = cf32(2); ld(gqt, gqk)
    gqbt = cf32(128); ld(gqbt, gqkb.rearrange("p a b -> p (a b)"))
    lamt = cf32(256); ld(lamt, lamb.rearrange("p a b -> p (a b)"))
    subl = cf32(128); ld(subl, sublnb)
    flag = cf32(1); ld(flag, flagin)
    wupF = cf32(512); ld(wupF, wup)
    aupF = cf32(512); ld(aupF, aup)
    gupF = cf32(512); ld(gupF, gup)
    identB = cbf(128)
    blkB = cbf(128)
    blk64B = cbf(128)
    permB = cbf(128)
    onesMS = cbf(128)
    blk64F = cf32(128)
    id64B = cbf(4 * 64)
    mask4 = cf32(2 * 4 * 128)
    maskt4 = cf32(2 * 4 * 64)
    wupB = cbf(512)
    aupB = cbf(512)
    gupB = cbf(512)
    omka = cf32(4)
    eps6 = cf32(1)
    epsgn = cf32(1)
    bsame = cf32(1)
    bcross = cf32(1)
    nlam = cf32(1)
    sl8 = cf32(128)
    gq8 = cf32(1)
    tmpc = cf32(64)
    tmpc2 = cf32(4)

    def cop(eng, fn):
        S.op(eng, fn, reads=[cb], writes=[cb])

    cop('dve', I('tensor_copy', out=identB, in_=identF))
    cop('dve', I('tensor_copy', out=blkB, in_=blkF))
    cop('dve', I('tensor_scalar', out=blk64B, in0=blkF, scalar1=1.0 / 64, scalar2=None, op0=ALU.mult))
    cop('dve', I('tensor_scalar', out=blk64F, in0=blkF, scalar1=1.0 / 64, scalar2=None, op0=ALU.mult))
    cop('dve', I('tensor_copy', out=permB, in_=permF))
    cop('dve', I('memset', onesMS, 1.0 / 1024))
    for hp in range(4):
        cop('dve', I('tensor_copy', out=id64B[:, hp * 64:(hp + 1) * 64], in_=id64F))
        for d in range(2):
            cop('dve', I('tensor_copy',
                out=mask4[:, (d * 4 + hp) * 128:(d * 4 + hp + 1) * 128], in_=maskF[:, d * 128:(d + 1) * 128]))
            cop('dve', I('tensor_copy',
                out=maskt4[:, (d * 4 + hp) * 64:(d * 4 + hp + 1) * 64], in_=masktF[:, d * 64:(d + 1) * 64]))
    cop('dve', I('tensor_copy', out=wupB, in_=wupF))
    cop('dve', I('tensor_copy', out=aupB, in_=aupF))
    cop('dve', I('tensor_copy', out=gupB, in_=gupF))
    cop('dve', I('tensor_scalar', out=omka, in0=rwt[:, 4:8], scalar1=-1.0, scalar2=1.0, op0=ALU.mult, op1=ALU.add))
    cop('dve', I('memset', eps6, 1e-6))
    cop('dve', I('memset', epsgn, GN_EPS))
    cop('dve', I('tensor_reduce', out=tmpc2[:, 0:2], in_=gqbt.rearrange("p (a b) -> p a b", a=2),
                                         axis=AX.X, op=ALU.max, apply_absolute_value=True))
    cop('dve', I('tensor_tensor', out=tmpc2[:, 2:3], in0=tmpc2[:, 0:1], in1=tmpc2[:, 1:2], op=ALU.mult))
    cop('dve', I('tensor_scalar', out=bsame, in0=tmpc2[:, 2:3], scalar1=-8.0, scalar2=None, op0=ALU.mult))
    cop('dve', I('tensor_scalar', out=tmpc2[:, 3:4], in0=flag, scalar1=NEG, scalar2=-NEG, op0=ALU.mult, op1=ALU.add))
    cop('dve', I('tensor_tensor', out=bcross, in0=bsame, in1=tmpc2[:, 3:4], op=ALU.add))
    lam3 = lamt.rearrange("p (a b) -> p a b", a=4)
    cop('dve', I('tensor_tensor', out=tmpc, in0=lam3[:, 0, :], in1=lam3[:, 1, :], op=ALU.mult))
    cop('dve', I('tensor_reduce', out=tmpc2[:, 0:1], in_=tmpc, axis=AX.X, op=ALU.add))
    cop('dve', I('tensor_tensor', out=tmpc, in0=lam3[:, 2, :], in1=lam3[:, 3, :], op=ALU.mult))
    cop('dve', I('tensor_reduce', out=tmpc2[:, 1:2], in_=tmpc, axis=AX.X, op=ALU.add))
    cop('act', I('activation', out=tmpc2[:, 0:2], in_=tmpc2[:, 0:2], func=AF.Exp))
    cop('dve', I('tensor_tensor', out=tmpc2[:, 2:3], in0=tmpc2[:, 1:2], in1=tmpc2[:, 0:1], op=ALU.subtract))
    cop('dve', I('tensor_scalar', out=nlam, in0=tmpc2[:, 2:3], scalar1=-0.2, scalar2=None, op0=ALU.add))
    cop('dve', I('tensor_scalar', out=sl8, in0=subl, scalar1=0.8, scalar2=None, op0=ALU.mult))
    cop('dve', I('tensor_scalar', out=gq8, in0=gqt[:, 0:1], scalar1=0.125, scalar2=None, op0=ALU.mult))

    wcbs = Rot([Buf() for _ in range(6)])

    def castw(dst, src, rows):
        for r0 in range(0, rows, 128):
            wcb = wcbs.next()
            S.dma('pool', I('dma_start', out=dst[r0:r0 + 128, :], in_=src[r0:r0 + 128, :]), wcb, writes=[wcb])

    import os
    NOCAST = os.environ.get("NOCAST")
    for i in range(2):
        wgr_ = wgu[i].rearrange("(k p) n -> p k n", p=128)
        for j in range(NJ):
            for g in range(2):
                wcb = wcbs.next()
                S.dma('pool', I('dma_start', out=wgu_t[i][j].rearrange("p (k g c) -> p k g c", k=8, g=2)[:, :, g, :],
                                in_=wgr_[:, :, g * DFF + j * 128: g * DFF + (j + 1) * 128]), wcb, writes=[wcb])
        wdr_ = wdn[i].rearrange("(j p) n -> p j n", p=128)
        for n in range(KC):
            wcb = wcbs.next()
            S.dma('pool', I('dma_start', out=wdn_t[i][n].rearrange("p (j c) -> p j c", j=NJ), in_=wdr_[:, :, n * 128:(n + 1) * 128]),
                  wcb, writes=[wcb])
    castw(win_bf, win, D)
    castw(wout_bf, wout, D)

    AR_WORDS = 46900
    ar = Arena(S, AR_WORDS)

    tapsrow = din("tapsrow", [128, 3, 1920])
    ar.reset()
    trow = ar.f32(3 * 1920); btr = Buf()
    S.dma('sp', I('dma_start', out=trow, in_=tapsrow.rearrange("p a b -> p (a b)")), btr, writes=[btr])
    wl = [(ar.f32(8 * 128), Buf()) for _ in range(2)]
    wo_ = [(ar.bf16(3 * 8 * 128), Buf()) for _ in range(2)]
    winr = win.rearrange("(kc p) n -> p kc n", p=128)
    for m in range(15):
        wt, wb_ = wl[m % 2]
        ot, ob = wo_[m % 2]
        wt3 = wt.rearrange("p (k c) -> p k c", k=8)
        S.dma('sp', I('dma_start', out=wt3, in_=winr[:, :, m * 128:(m + 1) * 128]), wb_, writes=[wb_])
        for d in range(3):
            o3 = ot[:, d * 1024:(d + 1) * 1024].rearrange("p (k c) -> p k c", k=8)
            tb = trow[:, d * 1920 + m * 128: d * 1920 + (m + 1) * 128]
            S.op('dve', I('tensor_tensor',
                out=o3, in0=wt3, in1=tb.unsqueeze(1).to_broadcast([128, 8, 128]), op=ALU.mult),
                reads=[wb_, btr], writes=[ob])
        S.dma('sp', I('dma_start', out=wconv[m], in_=ot), ob, reads=[ob])
    S.barrier()

    if stop == 'pro':
        return finish()

    def evac_copy(eng, dst, src, reads, writes):
        if eng == 'act':
            S.op('act', I('activation', out=dst, in_=src, func=AF.Copy), reads=reads, writes=writes)
        else:
            S.op(eng, I('tensor_copy', out=dst, in_=src), reads=reads, writes=writes)

    def rstd_from_ps(ps_ap, rdst, epst, reads, wb, width):
        S.op('act', I('activation', out=rdst, in_=ps_ap, func=AF.Ln, bias=epst[:, 0:1], scale=1.0),
             reads=reads + [cb], writes=[wb])
        S.op('act', I('activation', out=rdst, in_=rdst, func=AF.Exp, scale=-0.5), reads=[wb], writes=[wb])

    def rmsnorm_g(xT3, xbufs, which, out3, obuf, sq3, sqb, rst, rsb):
        for half in range(2):
            S.op('act', I('activation', out=sq3[:, half * 4:(half + 1) * 4, :], in_=xT3[:, half * 4:(half + 1) * 4, :], func=AF.Square),
                 reads=xbufs, writes=[sqb])
            yield
        ps, pb = psrot.next()
        for kc in range(KC):
            S.op('pe', I('matmul', ps[:, 0:T], lhsT=onesMS, rhs=sq3[:, kc, :], start=(kc == 0), stop=(kc == KC - 1)),
                 reads=[sqb, cb], writes=[pb])
        S.op('act', I('activation', out=rst, in_=ps[:, 0:T], func=AF.Ln, bias=eps6[:, 0:1], scale=1.0), reads=[pb, cb], writes=[rsb])
        yield
        S.op('act', I('activation', out=rst, in_=rst, func=AF.Exp, scale=-0.5), reads=[rsb], writes=[rsb])
        yield
        for kc in range(KC):
            S.op('dve', I('scalar_tensor_tensor', out=out3[:, kc, :], in0=xT3[:, kc, :], scalar=gv[:, which * 8 + kc: which * 8 + kc + 1], in1=rst,
                 op0=ALU.mult, op1=ALU.mult), reads=[xbufs[kc], rsb, cb], writes=[obuf])
            if kc % 2 == 1:
                yield

    def rmsnorm(*a_, **k_):
        for _ in rmsnorm_g(*a_, **k_):
            pass

    def ffn_g(xT3, xbufs, which_norm, li, R):
        hT3, hb = R['hT']
        yield from rmsnorm_g(xT3, xbufs, which_norm, hT3, hb, R['sq'][0], R['sq'][1], R['rst'][0], R['rst'][1])
        actT3, actb = R['actT']
        for j in range(NJ):
            wt, wb_ = R['wgu'].next()
            wt3 = wt.rearrange("p (k c) -> p k c", k=8)
            S.dma('sp' if j % 2 == 0 else 'act', I('dma_start', out=wt, in_=wgu_t[li][j]), wb_, writes=[wb_])
            pg, pgb = psrot.next()
            pu, pub = psrot.next()
            for kc in range(KC):
                S.op('pe', I('matmul', pg[:, 0:T], lhsT=wt3[:, kc, 0:128], rhs=hT3[:, kc, :],
                                                                     start=(kc == 0), stop=(kc == KC - 1)),
                     reads=[wb_, hb], writes=[pgb])
            for kc in range(KC):
                S.op('pe', I('matmul', pu[:, 0:T], lhsT=wt3[:, kc, 128:256], rhs=hT3[:, kc, :],
                                                                     start=(kc == 0), stop=(kc == KC - 1)),
                     reads=[wb_, hb], writes=[pub])
            sg, sgb = R['sg'].next()
            S.op('act', I('activation', out=sg, in_=pg[:, 0:T], func=AF.Silu), reads=[pgb], writes=[sgb])
            S.op('dve', I('tensor_tensor', out=actT3[:, j, :], in0=sg, in1=pu[:, 0:T], op=ALU.mult),
                 reads=[sgb, pub], writes=[actb[j]])
            yield
        for n in range(KC):
            wt, wb_ = R['wdn'].next()
            wt3 = wt.rearrange("p (j c) -> p j c", j=NJ)
            S.dma('sp' if n % 2 == 0 else 'act', I('dma_start', out=wt, in_=wdn_t[li][n]), wb_, writes=[wb_])
            pd, pdb = psrot.next()
            for j in range(NJ):
                S.op('pe', I('matmul', pd[:, 0:T], lhsT=wt3[:, j, :], rhs=actT3[:, j, :],
                                                                   start=(j == 0), stop=(j == NJ - 1)),
                     reads=[wb_, actb[j]], writes=[pdb])
            S.op('dve', I('scalar_tensor_tensor', out=xT3[:, n, :], in0=pd[:, 0:T], scalar=0.5, in1=xT3[:, n, :],
                                                                      op0=ALU.mult, op1=ALU.add),
                 reads=[pdb, xbufs[n]], writes=[xbufs[n]])
            yield

    def ffn(*a_, **k_):
        for _ in ffn_g(*a_, **k_):
            pass

    def ffn_resources(ngu=2, ndn=2):
        R = {}
        R['hT'] = (ar.bf16(8 * T).rearrange("p (k t) -> p k t", k=8), Buf())
        R['sq'] = (ar.bf16(8 * T).rearrange("p (k t) -> p k t", k=8), Buf())
        R['rst'] = (ar.f32(T), Buf())
        R['actT'] = (ar.bf16(NJ * T).rearrange("p (j t) -> p j t", j=NJ), [Buf() for _ in range(NJ)])
        R['wgu'] = Rot([(ar.bf16(8 * 256), Buf()) for _ in range(ngu)])
        R['wdn'] = Rot([(ar.bf16(NJ * 128), Buf()) for _ in range(ndn)])
        R['sg'] = Rot([(ar.f32(T), Buf()) for _ in range(2)])
        return R

    ar.reset()
    R = ffn_resources()
    xT3 = ar.f32(8 * T).rearrange("p (k t) -> p k t", k=8)
    xbufs = [Buf() for _ in range(KC)]
    xin = [(ar.f32(SUB * D).rearrange("p (s d) -> p s d", s=SUB), Buf()) for _ in range(1)]
    TE = T + 2
    h2e = [(ar.bf16(8 * TE).rearrange("p (k t) -> p k t", k=8), Buf()) for _ in range(4)]
    wcs = Rot([(ar.bf16(3 * 8 * 128), Buf()) for _ in range(2)])
    wv = (ar.bf16(8 * 512).rearrange("p (k c) -> p k c", k=8), Buf())
    ut = Rot([(ar.f32(T), Buf()) for _ in range(2)])
    cs = Rot([(ar.f32(2 * T), Buf()) for _ in range(1)])
    vo = Rot([(ar.bf16(512), Buf()) for _ in range(2)])

    winbr = win_bf.rearrange("(kc p) n -> p kc n", p=128)
    S.dma('sp', I('dma_start', out=wv[0], in_=winbr[:, :, 1920 + 1024: 3456]), wv[1], writes=[wv[1]])

    def qk_bufs():
        return dict(t1=(ar.f32(T), Buf()), t2=(ar.f32(T), Buf()), t3=(ar.f32(T), Buf()),
                    sq=(ar.bf16(T), Buf()), qnb=(ar.bf16(T), Buf()), qo=Rot([(ar.bf16(T), Buf()) for _ in range(2)]),
                    wqk=Rot([(ar.bf16(8 * 128).rearrange("p (k c) -> p k c", k=8), Buf()) for _ in range(1)]))

    QKB = [qk_bufs(), qk_bufs()]

    def proj_rw(i, he, heb):
        t0 = i * T
        for m in range(15):
            wt, wb_ = wcs.next()
            S.dma('sp', I('dma_start', out=wt, in_=wconv[m]), wb_, writes=[wb_])
            wt4 = wt.rearrange("p (d k c) -> p d k c", d=3, k=8)
            ps, pb = psrot.next()
            n = 0
            for d in range(3):
                for kc in range(KC):
                    S.op('pe', I('matmul', ps[:, 0:T], lhsT=wt4[:, d, kc, :], rhs=he[:, kc, d:d + T], start=(n == 0), stop=(n == 23)),
                         reads=[wb_, heb], writes=[pb])
                    n += 1
            u, ub = ut.next()
            evac_copy('act', u, ps[:, 0:T], [pb], [ub])
            S.dma('sp', I('dma_start', out=URW[m, :, t0:t0 + T], in_=u), ub, reads=[ub])
            yield
        for s_ in range(SUB):
            ps, pb = psrot.next()
            for kc in range(KC):
                S.op('pe', I('matmul', ps[:, 0:512], lhsT=he[:, kc, 1 + s_ * 128: 1 + (s_ + 1) * 128], rhs=wv[0][:, kc, :],
                             start=(kc == 0), stop=(kc == KC - 1)), reads=[wv[1], heb], writes=[pb])
            v_, vb_ = vo.next()
            evac_copy('act', v_, ps[:, 0:512], [pb], [vb_])
            S.dma('sp', I('dma_start', out=VDA[t0 + s_ * 128: t0 + (s_ + 1) * 128, :], in_=v_), vb_, reads=[vb_])
            yield

    def proj_qk(i, he, heb, ms, QB_, ct, ctb):
        t0 = i * T
        r_, rb_ = QB_['t1']; qn, qnbuf = QB_['t2']; tt, ttb = QB_['t3']
        sq, sqbb = QB_['sq']; qb_, qbb = QB_['qnb']
        for m in ms:
            isq = m < 4
            wq_, wqb_ = QB_['wqk'].next()
            S.dma('sp', I('dma_start', out=wq_, in_=winbr[:, :, 1920 + m * 128: 1920 + (m + 1) * 128]), wqb_, writes=[wqb_])
            ps, pb = psrot.next()
            for kc in range(KC):
                S.op('pe', I('matmul', ps[:, 0:T], lhsT=wq_[:, kc, :], rhs=he[:, kc, 1:1 + T],
                             start=(kc == 0), stop=(kc == KC - 1)), reads=[wqb_, heb], writes=[pb])
            S.op('act', I('activation', out=tt, in_=ps[:, 0:T], func=AF.Copy), reads=[pb], writes=[ttb])
            yield
            S.op('act', I('activation', out=sq, in_=tt, func=AF.Square), reads=[ttb], writes=[sqbb])
            yield
            ps2, pb2 = psrot.next()
            S.op('pe', I('matmul', ps2[:, 0:T], lhsT=blk64B, rhs=sq, start=True, stop=True), reads=[sqbb, cb], writes=[pb2])
            S.op('act', I('activation', out=r_, in_=ps2[:, 0:T], func=AF.Ln, bias=eps6[:, 0:1], scale=1.0), reads=[pb2, cb], writes=[rb_])
            yield
            S.op('act', I('activation', out=r_, in_=r_, func=AF.Exp, scale=-0.5), reads=[rb_], writes=[rb_])
            yield
            gsc = gq8 if isq else gqt[:, 1:2]
            S.op('dve', I('scalar_tensor_tensor', out=qn, in0=tt, scalar=gsc, in1=r_, op0=ALU.mult, op1=ALU.mult),
                 reads=[ttb, rb_, cb], writes=[qnbuf])
            yield
            S.op('pool', I('tensor_copy', out=qb_, in_=qn), reads=[qnbuf], writes=[qbb])
            yield
            ps3, pb3 = psrot.next()
            S.op('pe', I('matmul', ps3[:, 0:T], lhsT=permB, rhs=qb_, start=True, stop=True), reads=[qbb, cb], writes=[pb3])
            S.op('dve', I('tensor_tensor', out=tt, in0=ps3[:, 0:T], in1=ct[:, T:2 * T], op=ALU.mult), reads=[pb3, ctb, qnbuf], writes=[ttb])
            S.op('pool', I('tensor_tensor', out=qn, in0=qn, in1=ct[:, 0:T], op=ALU.mult), reads=[qnbuf, ctb], writes=[qnbuf])
            yield
            o_, ob_ = QB_['qo'].next()
            S.op('dve', I('tensor_tensor', out=o_, in0=qn, in1=tt, op=ALU.add), reads=[qnbuf, ttb], writes=[ob_])
            dst = QT if isq else KT
            S.dma('sp', I('dma_start', out=dst[m % 4, :, t0:t0 + T], in_=o_), ob_, reads=[ob_])
            yield

    def run_threads(gens):
        gens = list(gens)
        while gens:
            for g in list(gens):
                try:
                    next(g)
                except StopIteration:
                    gens.remove(g)

    def project_threads(i):
        t0 = i * T
        he, heb = h2e[i % len(h2e)]
        ct, ctb = cs.next()
        S.dma('sp', [I('dma_start', out=ct[:, 0:T], in_=costab[:, t0:t0 + T]),
                     I('dma_start', out=ct[:, T:2 * T], in_=sintab[:, t0:t0 + T])], ctb, writes=[ctb])
        return [proj_rw(i, he, heb), proj_qk(i, he, heb, [0, 4, 1, 5], QKB[0], ct, ctb), proj_qk(i, he, heb, [2, 6, 3, 7], QKB[1], ct, ctb)]

    NH = len(h2e)

    def tile_ffn(i):
        t0 = i * T
        xi, xib = xin[0]
        S.dma('sp', I('dma_start', out=xi, in_=x[t0:t0 + T, :].rearrange("(s p) d -> p s d", p=128)), xib, writes=[xib])
        for kc in range(KC):
            ps, pb = psrot.next()
            for s_ in range(SUB):
                S.op('pe', I('transpose', out=ps[:, s_ * 128:(s_ + 1) * 128], in_=xi[:, s_, kc * 128:(kc + 1) * 128], identity=identF),
                     reads=[xib, cb], writes=[pb])
            evac_copy('act' if kc % 2 else 'dve', xT3[:, kc, :], ps[:, 0:T], [pb], [xbufs[kc]])
            yield
        yield from ffn_g(xT3, xbufs, 0, 0, R)
        S.dma('sp', I('dma_start', out=X1T.rearrange("k p t -> p k t")[:, :, t0:t0 + T], in_=xT3), xbufs[0], reads=xbufs)
        he, heb = h2e[i % NH]
        yield from rmsnorm_g(xT3, xbufs, 1, he[:, :, 1:1 + T], heb, R['sq'][0], R['sq'][1], R['rst'][0], R['rst'][1])
        if i == 0:
            S.op('dve', I('memset', he[:, :, 0:1], 0.0), writes=[heb])
        else:
            hp_, hpb = h2e[(i - 1) % NH]
            if t0 % SEG == 0:
                S.op('dve', I('tensor_scalar', out=he[:, :, 0:1], in0=hp_[:, :, T:T + 1], scalar1=flag[:, 0:1], scalar2=None, op0=ALU.mult),
                     reads=[hpb, cb], writes=[heb])
                S.op('dve', I('tensor_scalar', out=hp_[:, :, T + 1:T + 2], in0=he[:, :, 1:2], scalar1=flag[:, 0:1], scalar2=None, op0=ALU.mult),
                     reads=[heb, cb], writes=[hpb])
            else:
                S.op('dve', I('tensor_copy', out=he[:, :, 0:1], in_=hp_[:, :, T:T + 1]), reads=[hpb], writes=[heb])
                S.op('dve', I('tensor_copy', out=hp_[:, :, T + 1:T + 2], in_=he[:, :, 1:2]), reads=[heb], writes=[hpb])
        if i == NTL - 1:
            S.op('dve', I('memset', he[:, :, T + 1:T + 2], 0.0), writes=[heb])
        yield

    for i in range(NTL + 2):
        threads = []
        if i < NTL:
            threads.append(tile_ffn(i))
        if 0 <= i - 2 < NTL:
            threads += project_threads(i - 2)
        run_threads(threads)
    S.barrier()

    if stop == 'A':
        return finish()
    ar.reset()
    KTs = ar.bf16(4 * NT).rearrange("p (h t) -> p h t", h=4); ktb = Buf()
    VA = ar.bf16(NKT * 4 * 130).rearrange("p (k h e) -> p k h e", k=NKT, h=4); vab = Buf()
    TQ = min(512, SEG, NT)
    NQT = NT // TQ
    QB = TQ // 128
    Qs = [(ar.bf16(4 * TQ).rearrange("p (h t) -> p h t", h=4), Buf()) for _ in range(2)]
    Es = Rot([(ar.bf16(2 * TQ), Buf()) for _ in range(3)])
    of = Rot([(ar.f32(128), Buf()) for _ in range(2)])
    on = Rot([(ar.f32(128), Buf()) for _ in range(2)])
    sm = Rot([(ar.f32(8), Buf()) for _ in range(2)])
    junk = (ar.f32(128), Buf())
    yo = Rot([(ar.bf16(TQ), Buf()) for _ in range(2)])
    for h in range(4):
        S.dma('sp', I('dma_start', out=KTs[:, h, :], in_=KT[h]), ktb, writes=[ktb])
    vsrc = VDA.rearrange("(k p) (h e) -> p k h e", p=128, h=4)
    for h in range(4):
        S.dma('sp', I('dma_start', out=VA[:, :, h, 0:128], in_=vsrc[:, :, h, :]), vab, writes=[vab])
    S.op('pool', I('memset', VA[:, :, :, 128:129], 1.0), writes=[vab])
    PSP = Rot([(PSD[0], PSB[0][1], PSB[1][1]), (PSD[1], PSB[2][1], PSB[3][1])])
    OB = [[(PSB[4 + (c * 4 + qb) // 2][0][:, ((c * 4 + qb) % 2) * 256: ((c * 4 + qb) % 2) * 256 + 129], PSB[4 + (c * 4 + qb) // 2][1])
           for qb in range(4)] for c in range(2)]
    osb = Rot([(ar.f32(8 * 132), Buf()) for _ in range(2)])
    LOOK = 1

    def finalize(h, q0):
        os_, osb_ = osb.next()
        o4 = os_.rearrange("p (c q e) -> p c q e", c=2, q=4)
        for c in range(2):
            for qb in range(QB):
                o_ap, o_b = OB[c][qb]
                S.op('dve', I('tensor_copy', out=o4[:, c, qb, 0:129], in_=o_ap), reads=[o_b], writes=[osb_])
        pd_, pstb, _pb1 = PSP.next()
        pst = pd_[:, 0:512]
        for qb in range(QB):
            o1 = o4[:, 0, qb, :]
            o2 = o4[:, 1, qb, :]
            s_, sb_ = sm.next()
            S.op('dve', I('reciprocal', out=s_[:, 0:1], in_=o1[:, 128:129]), reads=[osb_], writes=[sb_])
            S.op('dve', I('reciprocal', out=s_[:, 1:2], in_=o2[:, 128:129]), reads=[osb_], writes=[sb_])
            S.op('dve', I('tensor_tensor', out=s_[:, 1:2], in0=s_[:, 1:2], in1=nlam, op=ALU.mult), reads=[sb_, cb], writes=[sb_])
            f_, fb_ = of.next()
            S.op('dve', I('tensor_scalar', out=f_, in0=o1[:, 0:128], scalar1=s_[:, 0:1], scalar2=None, op0=ALU.mult),
                 reads=[osb_, sb_], writes=[fb_])
            S.op('dve', I('scalar_tensor_tensor', out=f_, in0=o2[:, 0:128], scalar=s_[:, 1:2], in1=f_, op0=ALU.mult, op1=ALU.add),
                 reads=[osb_, sb_, fb_], writes=[fb_])
            S.op('pool', I('tensor_tensor', out=junk[0], in0=f_, in1=f_, op=ALU.mult), reads=[fb_], writes=[junk[1]])
            S.op('dve', I('tensor_reduce', out=s_[:, 2:3], in_=junk[0], axis=AX.X, op=ALU.add), reads=[junk[1]], writes=[sb_])
            S.op('act', I('activation', out=s_[:, 3:4], in_=s_[:, 2:3], func=AF.Ln, bias=eps6[:, 0:1], scale=1.0 / 128),
                 reads=[sb_, cb], writes=[sb_])
            S.op('act', I('activation', out=s_[:, 3:4], in_=s_[:, 3:4], func=AF.Exp, scale=-0.5), reads=[sb_], writes=[sb_])
            n_, nb_ = on.next()
            S.op('dve', I('scalar_tensor_tensor', out=n_, in0=f_, scalar=s_[:, 3:4], in1=sl8, op0=ALU.mult, op1=ALU.mult),
                 reads=[fb_, sb_, cb], writes=[nb_])
            S.op('pe', I('transpose', out=pst[:, qb * 128:(qb + 1) * 128], in_=n_, identity=identF), reads=[nb_, cb], writes=[pstb])
        yt, ytb = yo.next()
        evac_copy('dve', yt, pst[:, 0:TQ], [pstb], [ytb])
        S.dma('sp', I('dma_start', out=YMIX[4 + h, :, q0:q0 + TQ], in_=yt), ytb, reads=[ytb])

    for qt in range(NQT):
        q0 = qt * TQ
        Qt, Qb = Qs[qt % 2]
        S.dma('sp', I('dma_start', out=Qt, in_=QT.rearrange("h p t -> p h t")[:, :, q0:q0 + TQ]), Qb, writes=[Qb])
        seq = [(h, kt) for h in range(4) for kt in range(NKT)]
        pend = {}
        for idx in range(len(seq) + LOOK):
            if idx < len(seq):
                h, kt = seq[idx]
                pd_, pb0, pb1 = PSP.next()
                p2 = pd_.rearrange("p (c n) -> p c n", c=2)
                for c in range(2):
                    S.op('pe', I('matmul', p2[:, c, 0:TQ], lhsT=KTs[c * 64:(c + 1) * 64, h, kt * 128:(kt + 1) * 128], rhs=Qt[c * 64:(c + 1) * 64, h, :],
                                 start=True, stop=True, tile_position=(c * 64, 0)), reads=[ktb, Qb], writes=[pb0 if c == 0 else pb1])
                same = (kt * 128) // SEG == q0 // SEG
                bias = bsame if same else bcross
                E, Eb = Es.next()
                E3 = E.rearrange("p (c n) -> p c n", c=2)
                S.op('act', I('activation', out=E3, in_=p2[:, :, 0:TQ], func=AF.Exp, bias=bias[:, 0:1], scale=1.0), reads=[pb0, pb1, cb], writes=[Eb])
                pend[idx] = (E3, Eb)
            j = idx - LOOK
            if j >= 0:
                h, kt = seq[j]
                E3, Eb = pend.pop(j)
                for c in range(2):
                    for qb in range(QB):
                        o_ap, o_b = OB[c][qb]
                        S.op('pe', I('matmul', o_ap, lhsT=E3[:, c, qb * 128:(qb + 1) * 128], rhs=VA[:, kt, h, 0:129],
                                     start=(kt == 0 and qb % 2 == 0), stop=(kt == NKT - 1 and (qb % 2 == 1 or qb == QB - 1))),
                             reads=[Eb, vab], writes=[o_b])
                if kt == NKT - 1:
                    finalize(h, q0)
    S.barrier()

    if stop == 'B1':
        return finish()
    ar.reset()
    TB = min(T, 256)
    NTB = NT // TB
    NCB = TB // 64
    T4 = 4 * TB

    def v3(ap):
        return ap.rearrange("p (h t) -> p h t", h=4)

    def v4(ap):
        return ap.rearrange("p (h c s) -> p h c s", h=4, c=NCB)

    def vc(ap):
        return ap.rearrange("p (g s) -> p g s", s=64)

    def hv(ap):
        return ap.rearrange("p (h t) -> p h t", h=4)

    def hsl(hs):
        return slice(hs * 64, (hs + 1) * 64)

    def headmm(out_fn, l_fn, r_fn, reads, writes, more=()):
        terms = [(l_fn, r_fn, reads)] + list(more)
        for h in range(8):
            hp_, hs = h // 2, h % 2
            for ti, (lf, rf, rd) in enumerate(terms):
                S.op('pe', I('matmul', out_fn(hp_, hs), lhsT=lf(hp_, hs), rhs=rf(hp_, hs), start=(ti == 0), stop=(ti == len(terms) - 1),
                             tile_position=(hs * 64, hs * 64)), reads=rd, writes=writes)

    def tb():
        return (ar.f32(T4), Buf())

    def tbh():
        return (ar.bf16(T4), Buf())

    scanm = tbh()
    scanf = (ar.f32(T4), Buf())
    S.dma('sp', I('dma_start', out=scanf[0], in_=cscan[:, 0:T4]), scanf[1], writes=[scanf[1]])
    S.op('dve', I('tensor_copy', out=scanm[0], in_=scanf[0]), reads=[scanf[1]], writes=[scanm[1]])
    shared = dict(SQ=tbh(), XB=tbh(), RIN=scanf)

    def lora_sig(dst, upB, inb, biasc, d):
        for hp_ in range(4):
            ps, pb = psrot.next()
            S.op('pe', I('matmul', ps[:, 0:TB], lhsT=upB[d * 64:(d + 1) * 64, hp_ * 128:(hp_ + 1) * 128],
                         rhs=inb[0][d * 64:(d + 1) * 64, :], start=True, stop=True, tile_position=(d * 64, 0)),
                 reads=[inb[1], cb], writes=[pb])
            S.op('act', I('activation', out=v3(dst[0])[:, hp_, :], in_=ps[:, 0:TB], func=AF.Sigmoid,
                          bias=biasc[:, d * 4 + hp_: d * 4 + hp_ + 1], scale=1.0), reads=[pb, cb], writes=[dst[1]])

    def kdir(dst, a_t, Kf, TMPa):
        for hp_ in range(4):
            S.op('dve', I('tensor_scalar', out=v3(TMPa[0])[:, hp_, :], in0=v3(a_t[0])[:, hp_, :],
                          scalar1=rwt[:, 4 + hp_:5 + hp_], scalar2=omka[:, hp_:hp_ + 1], op0=ALU.mult, op1=ALU.add),
                 reads=[a_t[1], cb], writes=[TMPa[1]])
        S.op('dve', I('tensor_tensor', out=dst[0], in0=Kf[0], in1=TMPa[0], op=ALU.mult), reads=[Kf[1], TMPa[1]], writes=[dst[1]])

    urw_r = URW.rearrange("m p t -> p m t")

    def alloc_dir():
        B = {}
        for nm_ in ('Rf', 'Kf', 'Vf', 'LW', 'Aa', 'KK', 'KD', 'Bv', 'CUM', 'E1'):
            B[nm_] = tb()
        B['EXC'] = B['Aa']
        B['E2'] = B['Kf']
        B['WDf'] = (ar.f32(TB), Buf()); B['ADf'] = (ar.f32(TB), Buf())
        B['twb'] = (ar.bf16(TB), Buf()); B['adb'] = (ar.bf16(TB), Buf())
        B['TOT'] = (ar.f32(4 * NCB), Buf()); B['EW'] = (ar.f32(4 * NCB), Buf())
        B['ARt'] = (ar.bf16(4 * NCB * 128), Buf()); B['BKt'] = (ar.bf16(4 * NCB * 128), Buf())
        B['VTt'] = tbh(); B['BHt'] = tbh(); B['KHt'] = tbh()
        B['Hf'] = (ar.f32(256), Buf()); B['Hb'] = (ar.bf16(256), Buf()); B['Ht'] = (ar.f32(256), Buf())
        B['SCB'] = [(ar.bf16(512), Buf()) for _ in range(NCB)]
        B['SCK'] = [(ar.bf16(512), Buf()) for _ in range(NCB)]
        B['PP'] = [Rot([(ar.bf16(256), Buf()) for _ in range(2)]) for _ in range(NCB)]
        B['PT'] = [Rot([(ar.bf16(256), Buf()) for _ in range(2)]) for _ in range(NCB)]
        B['TT'] = [Rot([(ar.bf16(256), Buf()) for _ in range(2)]) for _ in range(NCB)]
        B['Zb'] = Rot([(ar.bf16(256), Buf()) for _ in range(2)])
        B['Ub'] = Rot([(ar.bf16(256), Buf()) for _ in range(2)])
        B['Yc'] = Rot([(ar.f32(256), Buf()) for _ in range(2)])
        return B

    def to_tm(src, dstt):
        xb, xbb = shared['XB']
        S.op('pool', I('tensor_copy', out=xb, in_=src[0]), reads=[src[1]], writes=[xbb])
        x4 = v4(xb)
        d4 = v4(dstt[0])
        for hp_ in range(4):
            ps, pb = psrot.next()
            p3 = ps[:, 0:NCB * 64].rearrange("p (c s) -> p c s", c=NCB)
            for c in range(NCB):
                for hs in range(2):
                    S.op('pe', I('matmul', p3[hsl(hs), c, :], lhsT=x4[hsl(hs), hp_, c, :], rhs=identB[hsl(hs), hsl(hs)], start=True, stop=True,
                                 tile_position=(hs * 64, hs * 64)), reads=[xbb, cb], writes=[pb])
            evac_copy('act' if hp_ % 2 else 'dve', d4[:, hp_, :, :], p3, [pb], [dstt[1]])

    def rw_dir(dirn, B):
        Rf, Kf, Vf, LW, Aa, KK, KD, Bv, CUM, EXC, E1, E2 = (B[k] for k in ('Rf', 'Kf', 'Vf', 'LW', 'Aa', 'KK', 'KD', 'Bv', 'CUM', 'EXC', 'E1', 'E2'))
        WDf, ADf, twb, adb, TOT, EW, ARt, BKt, VTt, BHt, KHt, Hf, Hb, Ht = (B[k] for k in (
            'WDf', 'ADf', 'twb', 'adb', 'TOT', 'EW', 'ARt', 'BKt', 'VTt', 'BHt', 'KHt', 'Hf', 'Hb', 'Ht'))
        SQ = shared['SQ']; RIN = shared['RIN']; DC = LW
        S.op('dve', I('memset', Hf[0], 0.0), writes=[Hf[1]])
        S.op('dve', I('memset', Hb[0], 0.0), writes=[Hb[1]])
        yr = YRW[dirn].rearrange("m p t -> p m t")
        tiles = list(range(NTB)) if dirn == 0 else list(range(NTB - 1, -1, -1))
        for i in tiles:
            t0 = i * TB
            S.dma('sp', I('dma_start', out=v3(Rf[0]), in_=urw_r[:, 0:4, t0:t0 + TB]), Rf[1], writes=[Rf[1]])
            S.dma('sp', I('dma_start', out=v3(Kf[0]), in_=urw_r[:, 4:8, t0:t0 + TB]), Kf[1], writes=[Kf[1]])
            S.dma('sp', I('dma_start', out=v3(Vf[0]), in_=urw_r[:, 8:12, t0:t0 + TB]), Vf[1], writes=[Vf[1]])
            S.dma('sp', I('dma_start', out=WDf[0], in_=URW[12, :, t0:t0 + TB]), WDf[1], writes=[WDf[1]])
            S.dma('sp', I('dma_start', out=ADf[0], in_=URW[13, :, t0:t0 + TB]), ADf[1], writes=[ADf[1]])
            yield
            S.op('act', I('activation', out=twb[0], in_=WDf[0], func=AF.Tanh), reads=[WDf[1]], writes=[twb[1]])
            S.op('pool', I('tensor_copy', out=adb[0], in_=ADf[0]), reads=[ADf[1]], writes=[adb[1]])
            yield
            lora_sig(LW, wupB, twb, w0t, dirn)
            yield
            S.op('pool', I('tensor_scalar', out=LW[0], in0=LW[0], scalar1=-DECAY, scalar2=None, op0=ALU.mult), reads=[LW[1]], writes=[LW[1]])
            lora_sig(Aa, aupB, adb, a0t, dirn)
            yield
            for hp_ in range(4):
                S.op('dve', I('tensor_scalar', out=v3(KK[0])[:, hp_, :], in0=v3(Kf[0])[:, hp_, :], scalar1=rwt[:, hp_:hp_ + 1],
                              scalar2=None, op0=ALU.mult), reads=[Kf[1], cb], writes=[KK[1]])
            yield
            S.op('act', I('activation', out=SQ[0], in_=KK[0], func=AF.Square), reads=[KK[1]], writes=[SQ[1]])
            for hp_ in range(4):
                ps, pb = psrot.next()
                S.op('pe', I('matmul', ps[:, 0:TB], lhsT=blkB, rhs=v3(SQ[0])[:, hp_, :], start=True, stop=True), reads=[SQ[1], cb], writes=[pb])
                S.op('dve', I('tensor_scalar', out=v3(RIN[0])[:, hp_, :], in0=ps[:, 0:TB], scalar1=1e-24, scalar2=None, op0=ALU.max),
                     reads=[pb], writes=[RIN[1]])
            S.op('act', I('activation', out=RIN[0], in_=RIN[0], func=AF.Ln), reads=[RIN[1]], writes=[RIN[1]])
            S.op('act', I('activation', out=RIN[0], in_=RIN[0], func=AF.Exp, scale=-0.5), reads=[RIN[1]], writes=[RIN[1]])
            S.op('dve', I('tensor_tensor', out=KK[0], in0=KK[0], in1=RIN[0], op=ALU.mult), reads=[KK[1], RIN[1]], writes=[KK[1]])
            kdir(KD, Aa, Kf, RIN)
            yield
            S.op('pool', I('tensor_tensor', out=Bv[0], in0=KK[0], in1=Aa[0], op=ALU.mult), reads=[KK[1], Aa[1]], writes=[Bv[1]])
            S.op('dve', I('tensor_tensor_scan', out=CUM[0], data0=scanm[0], data1=LW[0], initial=0.0, op0=ALU.mult, op1=ALU.add),
                 reads=[scanm[1], LW[1]], writes=[CUM[1]])
            yield
            S.op('dve', I('tensor_copy', out=TOT[0], in_=vc(CUM[0])[:, :, 63]), reads=[CUM[1]], writes=[TOT[1]])
            totb = TOT[0].unsqueeze(2).to_broadcast([128, 4 * NCB, 64])
            yield
            if dirn == 1:
                S.op('dve', I('scalar_tensor_tensor', out=CUM[0], in0=CUM[0], scalar=-1.0, in1=LW[0], op0=ALU.mult, op1=ALU.add),
                     reads=[CUM[1], LW[1]], writes=[CUM[1]])
                yield
                S.op('dve', I('tensor_tensor', out=vc(CUM[0]), in0=vc(CUM[0]), in1=totb, op=ALU.add), reads=[CUM[1], TOT[1]], writes=[CUM[1]])
                yield
            S.op('pool', I('tensor_tensor', out=EXC[0], in0=CUM[0], in1=LW[0], op=ALU.subtract), reads=[CUM[1], LW[1]], writes=[EXC[1]])
            S.op('act', I('activation', out=EW[0], in_=TOT[0], func=AF.Exp), reads=[TOT[1]], writes=[EW[1]])
            S.op('act', I('activation', out=E1[0], in_=CUM[0], func=AF.Exp), reads=[CUM[1]], writes=[E1[1]])
            S.op('act', I('activation', out=E2[0], in_=CUM[0], func=AF.Exp, scale=-1.0), reads=[CUM[1]], writes=[E2[1]])
            yield
            S.op('dve', I('tensor_tensor', out=vc(DC[0]), in0=totb, in1=vc(CUM[0]), op=ALU.subtract), reads=[CUM[1], TOT[1], EXC[1]], writes=[DC[1]])
            S.op('act', I('activation', out=EXC[0], in_=EXC[0], func=AF.Exp), reads=[EXC[1]], writes=[EXC[1]])
            yield
            S.op('act', I('activation', out=DC[0], in_=DC[0], func=AF.Exp), reads=[DC[1]], writes=[DC[1]])
            AR5 = ARt[0].rearrange("p (h c s) -> p h c s", h=4, c=NCB)
            BK5 = BKt[0].rearrange("p (h c s) -> p h c s", h=4, c=NCB)
            S.op('dve', I('scalar_tensor_tensor', out=AR5[:, :, :, 0:64], in0=v4(KK[0]), scalar=-1.0, in1=v4(EXC[0]), op0=ALU.mult, op1=ALU.mult),
                 reads=[KK[1], EXC[1]], writes=[ARt[1]])
            S.op('pool', I('tensor_tensor', out=AR5[:, :, :, 64:128], in0=v4(Rf[0]), in1=v4(E1[0]), op=ALU.mult),
                 reads=[Rf[1], E1[1]], writes=[ARt[1]])
            yield
            S.op('dve', I('tensor_tensor', out=BK5[:, :, :, 0:64], in0=v4(Bv[0]), in1=v4(E2[0]), op=ALU.mult), reads=[Bv[1], E2[1]], writes=[BKt[1]])
            S.op('pool', I('tensor_tensor', out=BK5[:, :, :, 64:128], in0=v4(KD[0]), in1=v4(E2[0]), op=ALU.mult), reads=[KD[1], E2[1]], writes=[BKt[1]])
            yield
            S.op('dve', I('tensor_tensor', out=E1[0], in0=Bv[0], in1=DC[0], op=ALU.mult), reads=[Bv[1], DC[1], ARt[1]], writes=[E1[1]])
            yield
            to_tm(E1, BHt)
            yield
            S.op('dve', I('tensor_tensor', out=E2[0], in0=KD[0], in1=DC[0], op=ALU.mult), reads=[KD[1], DC[1], BKt[1]], writes=[E2[1]])
            yield
            to_tm(E2, KHt)
            yield
            to_tm(Vf, VTt)
            yield
            VT4 = v4(VTt[0]); BH4 = v4(BHt[0]); KH4 = v4(KHt[0])
            Hb3 = hv(Hb[0]); Hf3 = hv(Hf[0]); Ht3 = hv(Ht[0])
            EW3 = EW[0].rearrange("p (h c) -> p h c", h=4)
            chunks = list(range(NCB)) if dirn == 0 else list(range(NCB - 1, -1, -1))
            st = {}
            for c in chunks:
                ps, pb = psrot.next()
                p3 = ps[:, 0:512].rearrange("p (h t) -> p h t", h=4)
                headmm(lambda hp_, hs: p3[hsl(hs), hp_, :], lambda hp_, hs: BK5[hsl(hs), hp_, c, 0:64], lambda hp_, hs: AR5[hsl(hs), hp_, c, :],
                       [ARt[1], BKt[1]], [pb])
                scb, scbb = B['SCB'][c]
                S.op('dve', I('tensor_tensor', out=scb, in0=ps[:, 0:512], in1=mask4[:, dirn * 512:(dirn + 1) * 512], op=ALU.mult),
                     reads=[pb, cb], writes=[scbb])
                ps, pb = psrot.next()
                p3 = ps[:, 0:512].rearrange("p (h t) -> p h t", h=4)
                headmm(lambda hp_, hs: p3[hsl(hs), hp_, :], lambda hp_, hs: BK5[hsl(hs), hp_, c, 64:128], lambda hp_, hs: AR5[hsl(hs), hp_, c, :],
                       [ARt[1], BKt[1]], [pb])
                sck, sckb = B['SCK'][c]
                S.op('dve', I('tensor_tensor', out=sck, in0=ps[:, 0:512], in1=mask4[:, dirn * 512:(dirn + 1) * 512], op=ALU.mult),
                     reads=[pb, cb], writes=[sckb])
                ps, pb = psrot.next()
                p3 = ps[:, 0:256].rearrange("p (h t) -> p h t", h=4)
                headmm(lambda hp_, hs: p3[hsl(hs), hp_, :], lambda hp_, hs: AR5[hsl(hs), hp_, c, 0:64], lambda hp_, hs: BK5[hsl(hs), hp_, c, 0:64],
                       [ARt[1], BKt[1]], [pb])
                P, Pb_ = B['PP'][c].next()
                S.op('dve', I('tensor_tensor', out=P, in0=ps[:, 0:256], in1=maskt4[:, dirn * 256:(dirn + 1) * 256], op=ALU.mult),
                     reads=[pb, cb], writes=[Pb_])
                Pt, Ptb = B['PT'][c].next()
                S.op('pool', I('tensor_copy', out=hv(Pt), in_=hv512(scb)[:, :, 0:64]), reads=[scbb], writes=[Ptb])
                TT, TTb = B['TT'][c].next()
                S.op('pool', I('tensor_tensor', out=TT, in0=Pt, in1=id64B, op=ALU.add), reads=[Ptb, cb], writes=[TTb])
                st[c] = [P, Pb_, Pt, Ptb, TT, TTb]
                yield
            for k in range(1, 6):
                new = {}
                for c in chunks:
                    P, Pb_, Pt, Ptb, TT, TTb = st[c]
                    P3 = hv(P); Pt3 = hv(Pt)
                    ps, pb = psrot.next()
                    p3 = hv(ps[:, 0:256])
                    headmm(lambda hp_, hs: p3[hsl(hs), hp_, :], lambda hp_, hs: Pt3[hsl(hs), hp_, :], lambda hp_, hs: P3[hsl(hs), hp_, :],
                           [Pb_, Ptb], [pb])
                    Pn, Pnb = B['PP'][c].next()
                    evac_copy('act', Pn, ps[:, 0:256], [pb], [Pnb])
                    Ptn, Ptnb = Pt, Ptb
                    if k < 5:
                        ps2, pb2 = psrot.next()
                        p32 = hv(ps2[:, 0:256])
                        headmm(lambda hp_, hs: p32[hsl(hs), hp_, :], lambda hp_, hs: P3[hsl(hs), hp_, :], lambda hp_, hs: Pt3[hsl(hs), hp_, :],
                               [Pb_, Ptb], [pb2])
                        Ptn, Ptnb = B['PT'][c].next()
                        evac_copy('dve', Ptn, ps2[:, 0:256], [pb2], [Ptnb])
                    new[c] = (Pn, Pnb, Ptn, Ptnb)
                    yield
                for c in chunks:
                    P, Pb_, Pt, Ptb, TT, TTb = st[c]
                    Pn, Pnb, Ptn, Ptnb = new[c]
                    Pn3 = hv(Pn); TT3 = hv(TT)
                    ps3, pb3 = psrot.next()
                    p33 = hv(ps3[:, 0:256])
                    headmm(lambda hp_, hs: p33[hsl(hs), hp_, :], lambda hp_, hs: Pn3[hsl(hs), hp_, :], lambda hp_, hs: TT3[hsl(hs), hp_, :],
                           [Pnb, TTb], [pb3])
                    TTn, TTnb = B['TT'][c].next()
                    S.op('dve', I('tensor_tensor', out=TTn, in0=ps3[:, 0:256], in1=TT, op=ALU.add), reads=[pb3, TTb], writes=[TTnb])
                    st[c] = [Pn, Pnb, Ptn, Ptnb, TTn, TTnb]
                    yield
            for c in chunks:
                TT, TTb = st[c][4], st[c][5]
                TT3 = hv(TT)
                scb, scbb = B['SCB'][c]; sck, sckb = B['SCK'][c]
                scb3 = hv512(scb); sck3 = hv512(sck)
                ps, pb = psrot.next()
                pz = hv(ps[:, 0:256])
                headmm(lambda hp_, hs: pz[hsl(hs), hp_, :], lambda hp_, hs: AR5[hsl(hs), hp_, c, 0:64], lambda hp_, hs: Hb3[hsl(hs), hp_, :],
                       [ARt[1], Hb[1]], [pb],
                       more=[(lambda hp_, hs: sck3[hsl(hs), hp_, 0:64], lambda hp_, hs: VT4[hsl(hs), hp_, c, :], [sckb, VTt[1]])])
                zb, zbb = B['Zb'].next()
                evac_copy('act', zb, ps[:, 0:256], [pb], [zbb])
                zb3 = hv(zb)
                S.op('pool', I('tensor_tensor', out=Ht3, in0=Hf3, in1=EW3[:, :, c:c + 1].to_broadcast([128, 4, 64]), op=ALU.mult),
                     reads=[Hf[1], EW[1]], writes=[Ht[1]])
                yield
                ps, pb = psrot.next()
                pu_ = hv(ps[:, 0:256])
                headmm(lambda hp_, hs: pu_[hsl(hs), hp_, :], lambda hp_, hs: TT3[hsl(hs), hp_, :], lambda hp_, hs: zb3[hsl(hs), hp_, :],
                       [TTb, zbb], [pb])
                ub, ubb = B['Ub'].next()
                evac_copy('dve', ub, ps[:, 0:256], [pb], [ubb])
                ub3 = hv(ub)
                yield
                psh, pbh = psrot.next()
                ph = hv(psh[:, 0:256])
                headmm(lambda hp_, hs: ph[hsl(hs), hp_, :], lambda hp_, hs: BH4[hsl(hs), hp_, c, :], lambda hp_, hs: ub3[hsl(hs), hp_, :],
                       [BHt[1], ubb], [pbh],
                       more=[(lambda hp_, hs: KH4[hsl(hs), hp_, c, :], lambda hp_, hs: VT4[hsl(hs), hp_, c, :], [KHt[1], VTt[1]])])
                ps, pb = psrot.next()
                py = hv(ps[:, 0:256])
                headmm(lambda hp_, hs: py[hsl(hs), hp_, :], lambda hp_, hs: Hb3[hsl(hs), hp_, :], lambda hp_, hs: AR5[hsl(hs), hp_, c, 64:128],
                       [ARt[1], Hb[1]], [pb],
                       more=[(lambda hp_, hs: ub3[hsl(hs), hp_, :], lambda hp_, hs: scb3[hsl(hs), hp_, 64:128], [ubb, scbb]),
                             (lambda hp_, hs: VT4[hsl(hs), hp_, c, :], lambda hp_, hs: sck3[hsl(hs), hp_, 64:128], [VTt[1], sckb])])
                tok_end = (t0 + (c + 1) * 64) if dirn == 0 else (t0 + c * 64)
                if tok_end % SEG == 0:
                    S.op('dve', I('tensor_tensor', out=Hf[0], in0=Ht[0], in1=psh[:, 0:256], op=ALU.add), reads=[Ht[1], pbh], writes=[Hf[1]])
                    S.op('dve', I('tensor_scalar', out=Hf[0], in0=Hf[0], scalar1=flag[:, 0:1], scalar2=None, op0=ALU.mult),
                         reads=[Hf[1], cb], writes=[Hf[1]])
                    S.op('act', I('activation', out=Hb[0], in_=Hf[0], func=AF.Copy), reads=[Hf[1]], writes=[Hb[1]])
                else:
                    S.op('dve', I('tensor_tensor', out=Hb[0], in0=Ht[0], in1=psh[:, 0:256], op=ALU.add), reads=[Ht[1], pbh], writes=[Hb[1]])
                    S.op('dve', I('tensor_tensor', out=Hf[0], in0=Ht[0], in1=psh[:, 0:256], op=ALU.add), reads=[Ht[1], pbh], writes=[Hf[1]])
                yc, ycb = B['Yc'].next()
                evac_copy('act', yc, ps[:, 0:256], [pb], [ycb])
                S.dma('sp', I('dma_start', out=yr[:, :, t0 + c * 64: t0 + (c + 1) * 64], in_=hv(yc)), ycb, reads=[ycb])
                yield

    def hv512(ap):
        return ap.rearrange("p (h t) -> p h t", h=4)

    def run_threads(gens):
        gens = list(gens)
        while gens:
            for g in list(gens):
                try:
                    next(g)
                except StopIteration:
                    gens.remove(g)

    Bd = [alloc_dir(), alloc_dir()]
    run_threads([rw_dir(0, Bd[0]), rw_dir(1, Bd[1])])
    S.barrier()

    ar.reset()
    TE_ = T
    E4 = 4 * TE_

    def e3(ap):
        return ap.rearrange("p (h t) -> p h t", h=4)

    def alloc_epi():
        B = {}
        for nm_ in ('Rf', 'Kf', 'Vf', 'Yt', 'Yp', 'Aa', 'TMPa', 'KD0', 'KD1', 'GG'):
            B[nm_] = (ar.f32(E4), Buf())
        B['ADf'] = (ar.f32(TE_), Buf()); B['GDf'] = (ar.f32(TE_), Buf())
        B['adb'] = (ar.bf16(TE_), Buf()); B['sgb'] = (ar.bf16(TE_), Buf())
        B['YO'] = (ar.bf16(E4), Buf())
        return B

    def lora_sig_e(dst, upB, inb, biasc, d):
        for hp_ in range(4):
            ps, pb = psrot.next()
            S.op('pe', I('matmul', ps[:, 0:TE_], lhsT=upB[d * 64:(d + 1) * 64, hp_ * 128:(hp_ + 1) * 128],
                         rhs=inb[0][d * 64:(d + 1) * 64, :], start=True, stop=True, tile_position=(d * 64, 0)),
                 reads=[inb[1], cb], writes=[pb])
            S.op('act', I('activation', out=e3(dst[0])[:, hp_, :], in_=ps[:, 0:TE_], func=AF.Sigmoid,
                          bias=biasc[:, d * 4 + hp_: d * 4 + hp_ + 1], scale=1.0), reads=[pb, cb], writes=[dst[1]])

    def kdir_e(dst, a_t, Kf, TMPa):
        for hp_ in range(4):
            S.op('dve', I('tensor_scalar', out=e3(TMPa[0])[:, hp_, :], in0=e3(a_t[0])[:, hp_, :],
                          scalar1=rwt[:, 4 + hp_:5 + hp_], scalar2=omka[:, hp_:hp_ + 1], op0=ALU.mult, op1=ALU.add),
                 reads=[a_t[1], cb], writes=[TMPa[1]])
        S.op('pool', I('tensor_tensor', out=dst[0], in0=Kf[0], in1=TMPa[0], op=ALU.mult), reads=[Kf[1], TMPa[1]], writes=[dst[1]])

    def rw_epi(tiles, B):
        Rf, Kf, Vf, Yt, Yp, Aa, TMPa, KD0, KD1, GG, ADf, GDf, adb, sgb_, YO = (B[k] for k in (
            'Rf', 'Kf', 'Vf', 'Yt', 'Yp', 'Aa', 'TMPa', 'KD0', 'KD1', 'GG', 'ADf', 'GDf', 'adb', 'sgb', 'YO'))
        yr0 = YRW[0].rearrange("m p t -> p m t")
        yr1 = YRW[1].rearrange("m p t -> p m t")
        for i in tiles:
            t0 = i * TE_
            S.dma('sp', I('dma_start', out=e3(Rf[0]), in_=urw_r[:, 0:4, t0:t0 + TE_]), Rf[1], writes=[Rf[1]])
            S.dma('sp', I('dma_start', out=e3(Kf[0]), in_=urw_r[:, 4:8, t0:t0 + TE_]), Kf[1], writes=[Kf[1]])
            S.dma('sp', I('dma_start', out=e3(Vf[0]), in_=urw_r[:, 8:12, t0:t0 + TE_]), Vf[1], writes=[Vf[1]])
            S.dma('sp', I('dma_start', out=ADf[0], in_=URW[13, :, t0:t0 + TE_]), ADf[1], writes=[ADf[1]])
            S.dma('sp', I('dma_start', out=GDf[0], in_=URW[14, :, t0:t0 + TE_]), GDf[1], writes=[GDf[1]])
            S.dma('sp', I('dma_start', out=e3(Yt[0]), in_=yr0[:, :, t0:t0 + TE_]), Yt[1], writes=[Yt[1]])
            S.dma('sp', I('dma_start', out=e3(Yp[0]), in_=yr1[:, :, t0:t0 + TE_]), Yp[1], writes=[Yp[1]])
            yield
            S.op('pool', I('tensor_copy', out=adb[0], in_=ADf[0]), reads=[ADf[1]], writes=[adb[1]])
            S.op('act', I('activation', out=sgb_[0], in_=GDf[0], func=AF.Sigmoid), reads=[GDf[1]], writes=[sgb_[1]])
            S.op('dve', I('tensor_tensor', out=Yt[0], in0=Yt[0], in1=Yp[0], op=ALU.add), reads=[Yt[1], Yp[1]], writes=[Yt[1]])
            yield
            lora_sig_e(Aa, aupB, adb, a0t, 0)
            yield
            kdir_e(KD0, Aa, Kf, TMPa)
            yield
            lora_sig_e(Aa, aupB, adb, a0t, 1)
            yield
            kdir_e(KD1, Aa, Kf, TMPa)
            yield
            for hp_ in range(4):
                ps, pb = psrot.next()
                S.op('pe', I('matmul', ps[:, 0:TE_], lhsT=blk64F, rhs=e3(Yt[0])[:, hp_, :], start=True, stop=True), reads=[Yt[1], cb], writes=[pb])
                S.op('dve', I('tensor_tensor', out=e3(Yp[0])[:, hp_, :], in0=e3(Yt[0])[:, hp_, :], in1=ps[:, 0:TE_], op=ALU.subtract),
                     reads=[Yt[1], pb], writes=[Yp[1]])
            yield
            S.op('act', I('activation', out=GG[0], in_=Yp[0], func=AF.Square), reads=[Yp[1]], writes=[GG[1]])
            S.op('pool', I('tensor_tensor', out=KD0[0], in0=KD0[0], in1=KD1[0], op=ALU.add), reads=[KD0[1], KD1[1]], writes=[KD0[1]])
            yield
            S.op('pool', I('tensor_tensor', out=KD0[0], in0=KD0[0], in1=Rf[0], op=ALU.mult), reads=[KD0[1], Rf[1]], writes=[KD0[1]])
            for hp_ in range(4):
                ps, pb = psrot.next()
                S.op('pe', I('matmul', ps[:, 0:TE_], lhsT=blk64F, rhs=e3(GG[0])[:, hp_, :], start=True, stop=True), reads=[GG[1], cb], writes=[pb])
                S.op('act', I('activation', out=e3(TMPa[0])[:, hp_, :], in_=ps[:, 0:TE_], func=AF.Ln, bias=epsgn[:, 0:1], scale=1.0),
                     reads=[pb, cb], writes=[TMPa[1]])
            yield
            S.op('act', I('activation', out=TMPa[0], in_=TMPa[0], func=AF.Exp, scale=-0.5), reads=[TMPa[1]], writes=[TMPa[1]])
            for hp_ in range(4):
                S.op('pool', I('tensor_scalar', out=e3(KD0[0])[:, hp_, :], in0=e3(KD0[0])[:, hp_, :], scalar1=rwt[:, 8 + hp_:9 + hp_],
                               scalar2=None, op0=ALU.mult), reads=[KD0[1], cb], writes=[KD0[1]])
            yield
            S.op('dve', I('tensor_tensor', out=Yp[0], in0=Yp[0], in1=TMPa[0], op=ALU.mult), reads=[Yp[1], TMPa[1]], writes=[Yp[1]])
            yield
            for hp_ in range(4):
                S.op('dve', I('tensor_scalar', out=e3(Yp[0])[:, hp_, :], in0=e3(Yp[0])[:, hp_, :], scalar1=rwt[:, 12 + hp_:13 + hp_],
                              scalar2=rwt[:, 16 + hp_:17 + hp_], op0=ALU.mult, op1=ALU.add), reads=[Yp[1], cb], writes=[Yp[1]])
            yield
            for hp_ in range(4):
                ps, pb = psrot.next()
                S.op('pe', I('matmul', ps[:, 0:TE_], lhsT=blkF, rhs=e3(KD0[0])[:, hp_, :], start=True, stop=True), reads=[KD0[1], cb], writes=[pb])
                S.op('dve', I('tensor_tensor', out=e3(GG[0])[:, hp_, :], in0=ps[:, 0:TE_], in1=e3(Vf[0])[:, hp_, :], op=ALU.mult),
                     reads=[pb, Vf[1]], writes=[GG[1]])
                ps2, pb2 = psrot.next()
                S.op('pe', I('matmul', ps2[:, 0:TE_], lhsT=gupB[:, hp_ * 128:(hp_ + 1) * 128], rhs=sgb_[0], start=True, stop=True),
                     reads=[sgb_[1], cb], writes=[pb2])
                S.op('pool', I('tensor_tensor', out=e3(GG[0])[:, hp_, :], in0=e3(GG[0])[:, hp_, :], in1=e3(Yp[0])[:, hp_, :], op=ALU.add),
                     reads=[GG[1], Yp[1]], writes=[GG[1]])
                S.op('dve', I('tensor_tensor', out=e3(YO[0])[:, hp_, :], in0=e3(GG[0])[:, hp_, :], in1=ps2[:, 0:TE_], op=ALU.mult),
                     reads=[GG[1], pb2], writes=[YO[1]])
                yield
            S.dma('sp', I('dma_start', out=YMIX.rearrange("m p t -> p m t")[:, 0:4, t0:t0 + TE_], in_=e3(YO[0])), YO[1], reads=[YO[1]])
            yield

    Be = [alloc_epi(), alloc_epi()]
    run_threads([rw_epi(list(range(0, NTL, 2)), Be[0]), rw_epi(list(range(1, NTL, 2)), Be[1])])
    S.barrier()

    if stop == 'B2':
        return finish()
    ar.reset()
    R = ffn_resources(4, 3)
    xT3 = ar.f32(8 * T).rearrange("p (k t) -> p k t", k=8)
    xbufs = [Buf() for _ in range(KC)]
    ym = [(ar.bf16(8 * T).rearrange("p (k t) -> p k t", k=8), Buf()) for _ in range(2)]
    wo = (ar.bf16(8 * D).rearrange("p (k c) -> p k c", k=8), Buf())
    fT = (ar.f32(8 * T).rearrange("p (k t) -> p k t", k=8), Buf())
    yout = [(ar.f32(SUB * D).rearrange("p (s d) -> p s d", s=SUB), Buf()) for _ in range(2)]
    S.dma('sp', I('dma_start', out=wo[0], in_=wout_bf.rearrange("(kc p) n -> p kc n", p=128)), wo[1], writes=[wo[1]])
    for i in range(NTL):
        t0 = i * T
        S.dma('sp', I('dma_start', out=xT3, in_=X1T.rearrange("k p t -> p k t")[:, :, t0:t0 + T]), xbufs[0], writes=xbufs)
        ymt, ymb = ym[i % 2]
        S.dma('sp', I('dma_start', out=ymt, in_=YMIX.rearrange("k p t -> p k t")[:, :, t0:t0 + T]), ymb, writes=[ymb])
        for n in range(KC):
            ps, pb = psrot.next()
            for kc in range(KC):
                S.op('pe', I('matmul', ps[:, 0:T], lhsT=wo[0][:, kc, n * 128:(n + 1) * 128], rhs=ymt[:, kc, :],
                                                                          start=(kc == 0), stop=(kc == KC - 1)), reads=[wo[1], ymb], writes=[pb])
            S.op('dve', I('tensor_tensor', out=xT3[:, n, :], in0=xT3[:, n, :], in1=ps[:, 0:T], op=ALU.add),
                 reads=[pb, xbufs[n]], writes=[xbufs[n]])
        ffn(xT3, xbufs, 2, 1, R)
        rmsnorm(xT3, xbufs, 3, fT[0], fT[1], R['sq'][0], R['sq'][1], R['rst'][0], R['rst'][1])
        yo_, yob = yout[i % 2]
        for s in range(SUB):
            for g in range(2):
                ps, pb = psrot.next()
                for k4 in range(4):
                    kc = g * 4 + k4
                    S.op('pe', I('transpose', out=ps[:, k4 * 128:(k4 + 1) * 128], in_=fT[0][:, kc, s * 128:(s + 1) * 128],
                                                                               identity=identF), reads=[fT[1], cb], writes=[pb])
                evac_copy('act' if g else 'dve', yo_[:, s, g * 512:(g + 1) * 512], ps[:, 0:512], [pb], [yob])
        S.dma('sp', I('dma_start', out=y[t0:t0 + T, :].rearrange("(s p) d -> p s d", p=128), in_=yo_), yob, reads=[yob])
    S.emit()
    S.close()
    return nc


def _rope_tabs(NT, SEG_POS):
    ROT = 16
    inv = (500000.0 ** (-np.arange(0, ROT, 2, dtype=np.float32) / ROT)).astype(np.float32)
    pos = (np.arange(NT) % SEG_POS).astype(np.float32)
    ang = pos[:, None] * inv[None, :]
    c = np.cos(ang).astype(np.float32)
    s = np.sin(ang).astype(np.float32)
    cos = np.ones((128, NT), np.float32)
    sin = np.zeros((128, NT), np.float32)
    for half in range(2):
        for i in range(8):
            cos[half * 64 + i] = c[:, i]
            cos[half * 64 + 8 + i] = c[:, i]
            sin[half * 64 + i] = -s[:, i]
            sin[half * 64 + 8 + i] = s[:, i]
    return cos, sin


def _consts(T):
    ident = np.eye(128, dtype=np.float32)
    blk = np.zeros((128, 128), np.float32)
    blk[:64, :64] = 1
    blk[64:, 64:] = 1
    perm = np.zeros((128, 128), np.float32)
    for half in range(2):
        for i in range(8):
            perm[half * 64 + i + 8, half * 64 + i] = 1
            perm[half * 64 + i, half * 64 + i + 8] = 1
    s = np.arange(128) % 64
    t = np.arange(64)
    mask = np.zeros((128, 2, 128), np.float32)
    mask[:, 0, 0:64] = (s[:, None] < t[None, :])
    mask[:, 0, 64:128] = (s[:, None] <= t[None, :])
    mask[:, 1, 0:64] = (s[:, None] > t[None, :])
    mask[:, 1, 64:128] = (s[:, None] >= t[None, :])
    maskt = np.zeros((128, 2, 64), np.float32)
    maskt[:, 0, :] = (s[:, None] > t[None, :])
    maskt[:, 1, :] = (s[:, None] < t[None, :])
    id64 = (s[:, None] == t[None, :]).astype(np.float32)
    scan = np.ones((128, 4 * T), np.float32)
    scan[:, ::64] = 0
    return dict(cident=ident, cblk=blk, cperm=perm, cmask=mask, cmaskt=maskt, cid64=id64, cscan=scan)


def _col(v, n):
    return np.ascontiguousarray(np.asarray(v, np.float32).reshape(n, 128).T)


def make_inputs(inp, xs, flags, segpos, NT, T):
    f = lambda k: np.asarray(inp[k], np.float32)[0]
    shared = {}
    shared['wgu1'] = np.ascontiguousarray(f('ffn1_w_gu'))
    shared['wd1'] = np.ascontiguousarray(f('ffn1_w_down'))
    shared['wgu2'] = np.ascontiguousarray(f('ffn2_w_gu'))
    shared['wd2'] = np.ascontiguousarray(f('ffn2_w_down'))
    shared['win'] = np.ascontiguousarray(f('w_in'))
    shared['wout'] = np.ascontiguousarray(f('w_out'))
    shared['gvec'] = np.ascontiguousarray(np.stack([_col(f('ffn1_norm'), 8), _col(f('mix_norm'), 8), _col(f('ffn2_norm'), 8),
                                                    _col(f('final_norm'), 8)], axis=1))
    cw = f('conv_w')
    shared['taps'] = np.ascontiguousarray(cw.reshape(3, 15, 128).transpose(2, 0, 1))
    shared['tapsrow'] = np.ascontiguousarray(np.broadcast_to(cw[None], (128, 3, 1920)))
    shared['w0c'] = np.ascontiguousarray(f('rw_w0').reshape(2, 4, 128).transpose(2, 0, 1))
    shared['a0c'] = np.ascontiguousarray(f('rw_a0').reshape(2, 4, 128).transpose(2, 0, 1))
    shared['wup'] = np.ascontiguousarray(f('rw_w_up').reshape(128, 512))
    shared['aup'] = np.ascontiguousarray(f('rw_a_up').reshape(128, 512))
    shared['gup'] = np.ascontiguousarray(f('rw_g_up'))
    shared['rwv'] = np.ascontiguousarray(np.stack([_col(f('rw_k_k'), 4), _col(f('rw_k_a'), 4), _col(f('rw_r_k').reshape(-1), 4),
                                                   _col(f('rw_ln_g'), 4), _col(f('rw_ln_b'), 4)], axis=1))
    gq = f('da_q_norm')
    gk = f('da_k_norm')
    shared['gqk'] = np.ascontiguousarray(np.stack([np.tile(gq, 2), np.tile(gk, 2)], axis=1))
    shared['gqkb'] = np.ascontiguousarray(np.broadcast_to(np.stack([gq, gk])[None], (128, 2, 64)))
    shared['lamb'] = np.ascontiguousarray(np.broadcast_to(np.stack([f('da_lq1'), f('da_lk1'), f('da_lq2'), f('da_lk2')])[None], (128, 4, 64)))
    shared['sublnb'] = np.ascontiguousarray(np.broadcast_to(f('da_subln')[None], (128, 128)))
    shared.update(_consts(T))
    maps = []
    for c, xc in enumerate(xs):
        m = dict(shared)
        m['x'] = np.ascontiguousarray(xc, dtype=np.float32)
        cos, sin = _rope_tabs(NT, segpos[c])
        m['costab'] = cos
        m['sintab'] = sin
        m['flag'] = np.full((128, 1), flags[c], np.float32)
        maps.append(m)
    return maps


_NC_CACHE = {}


def kernel(**inputs):
    NT, SEG, T = 8192, 2048, 512
    xp = np.asarray(inputs['x_prompt'], np.float32)
    xs_ = np.asarray(inputs['x_sample'], np.float32)
    xs = [xp[b] for b in range(4)] + [xs_[4 * c:4 * c + 4].reshape(NT, D) for c in range(4)]
    flags = [1.0] * 4 + [0.0] * 4
    segpos = [8192] * 4 + [2048] * 4
    maps = make_inputs(inputs, xs, flags, segpos, NT, T)
    key = (NT, SEG, T)
    if key not in _NC_CACHE:
        _NC_CACHE[key] = build(NT, SEG, T)
    nc = _NC_CACHE[key]
    res = run_bass_kernel_spmd(nc, maps, core_ids=list(range(8)))
    outs = [np.asarray(r['y'], np.float32) for r in res.results]
    y_prompt = np.stack(outs[0:4], axis=0)
    y_sample = np.concatenate([o.reshape(4, 2048, D) for o in outs[4:8]], axis=0)
    return (y_prompt, y_sample)
```

```python
import contextlib
import numpy as np
import concourse.bass as bass
import concourse.mybir as mybir
from concourse.bass_utils import run_bass_kernel_spmd

F32 = mybir.dt.float32
BF16 = mybir.dt.bfloat16
AF = mybir.ActivationFunctionType
ALU = mybir.AluOpType
AX = mybir.AxisListType

ENGS = ['pe', 'act', 'dve', 'pool', 'sp']
D = 1024
DFF = 2816
NJ = 22
KC = 8
DECAY = 0.606531
GN_EPS = 64e-5
NEG = 30000.0


def I(name, *a, **kw):
    return (name, a, kw)


class Buf:
    __slots__ = ('w', 'rs', 'dsem')

    def __init__(self):
        self.w = None
        self.rs = []
        self.dsem = None


DEBUGLOG = None


class Sched:
    def __init__(self, nc):
        self.nc = nc
        self.ops = {e: [] for e in ENGS}
        self.known = {e: {} for e in ENGS}
        self.ndsem = 0
        self.dcount = {}
        self.stack = contextlib.ExitStack()
        self.ntens = 0
        self.pending = {}

    def sb(self, shape, dtype, name=None):
        self.ntens += 1
        return self.stack.enter_context(self.nc.sbuf_tensor(f"t{self.ntens}", list(shape), dtype))

    def ps(self, shape, dtype):
        self.ntens += 1
        return self.stack.enter_context(self.nc.psum_tensor(f"p{self.ntens}", list(shape), dtype))

    def _deps(self, reads, writes):
        deps = []
        for b in reads:
            if b.w is not None:
                deps.append(b.w)
        for b in writes:
            if b.w is not None:
                deps.append(b.w)
            deps.extend(b.rs)
        return deps

    def _filter(self, eng, deps, force_same=False):
        kn = self.known[eng]
        best = {}
        for t in deps:
            if t[0] == 'e' and t[1] == eng and eng == 'pe' and not force_same:
                continue
            k = (t[0], t[1])
            if k not in best or best[k][2] < t[2]:
                best[k] = t
        waits = []
        for k, t in best.items():
            if t[0] == 'e':
                if kn.get(t[1], -1) >= t[2]:
                    continue
                kn[t[1]] = t[2]
                self.ops[t[1]][t[2]]['inc'] = True
            else:
                if kn.get(k, 0) >= t[2]:
                    continue
                kn[k] = t[2]
            waits.append(t)
        return waits

    def op(self, eng, fn, reads=(), writes=(), extra=(), force_same=False):
        deps = self._deps(reads, writes) + list(extra) + self.pending.pop(eng, [])
        waits = self._filter(eng, deps, force_same)
        idx = len(self.ops[eng])
        self.ops[eng].append(dict(fn=fn, waits=waits, inc=False, dma=None))
        tok = ('e', eng, idx)
        for b in reads:
            b.rs.append(tok)
            if len(b.rs) > 64:
                b.rs = self._compact(b.rs)
        for b in writes:
            b.w = tok
            b.rs = []
        return tok

    @staticmethod
    def _compact(rs):
        best = {}
        for t in rs:
            k = (t[0], t[1])
            if k not in best or best[k][2] < t[2]:
                best[k] = t
        return list(best.values())

    def dma(self, eng, fns, sembuf, reads=(), writes=()):
        if isinstance(fns, tuple):
            fns = [fns]
        if sembuf.dsem is None:
            sembuf.dsem = self.ndsem
            self.dcount[self.ndsem] = 0
            self.ndsem += 1
        sem = sembuf.dsem
        deps = self._deps(reads, writes) + self.pending.pop(eng, [])
        if self.dcount[sem] > 0:
            deps.append(('d', sem, self.dcount[sem]))
        waits = self._filter(eng, deps)
        self.dcount[sem] += 16 * len(fns)
        tok = ('d', sem, self.dcount[sem])
        self.ops[eng].append(dict(fn=fns, waits=waits, inc=False, dma=sem))
        for b in reads:
            b.rs.append(tok)
        for b in writes:
            b.w = tok
            b.rs = []
        return tok

    def barrier(self):
        toks = []
        for e in ENGS:
            n = len(self.ops[e])
            while n > 0 and self.ops[e][n - 1]['dma'] is not None:
                n -= 1
            if n > 0:
                toks.append(('e', e, n - 1))
        for sem, cnt in self.dcount.items():
            if cnt > 0:
                toks.append(('d', sem, cnt))
        for e in ENGS:
            self.pending[e] = list(toks)

    def emit(self):
        nc = self.nc
        fin = []
        for sem, cnt in self.dcount.items():
            if cnt > 0:
                fin.append(('d', sem, cnt))
        self.ops['sp'].append(dict(fn=None, waits=fin, inc=False, dma=None))
        vals = {}
        for e in ENGS:
            c = 0
            v = []
            for o in self.ops[e]:
                if o['inc']:
                    c += 1
                v.append(c)
            vals[e] = v
        st = self.stack
        esem = {e: st.enter_context(nc.semaphore(f"es_{e}")) for e in ENGS}
        dsem = [st.enter_context(nc.semaphore(f"ds_{i}")) for i in range(self.ndsem)]
        block = st.enter_context(nc.Block())

        def run(e, engobj):
            for o in self.ops[e]:
                for t in o['waits']:
                    if t[0] == 'e':
                        engobj.wait_ge(esem[t[1]], vals[t[1]][t[2]])
                    else:
                        engobj.wait_ge(dsem[t[1]], t[2])
                if o['fn'] is None:
                    continue
                if o['dma'] is not None:
                    for f in o['fn']:
                        _i = getattr(engobj, f[0])(*f[1], **f[2])
                        _i.then_inc(dsem[o['dma']], 16)
                        if DEBUGLOG is not None:
                            DEBUGLOG.append((e, _i.ins.name, str(f[2].get('out'))[:200], str(f[2].get('in_'))[:200]))
                else:
                    f = o['fn']
                    ins = getattr(engobj, f[0])(*f[1], **f[2])
                    if o['inc']:
                        ins.then_inc(esem[e], 1)

        @block.tensor
        def _(eng):
            run('pe', eng)

        @block.scalar
        def _(eng):
            run('act', eng)

        @block.vector
        def _(eng):
            run('dve', eng)

        @block.gpsimd
        def _(eng):
            run('pool', eng)

        @block.sync
        def _(eng):
            run('sp', eng)

    def close(self):
        self.stack.close()


class Arena:
    def __init__(self, S, words):
        self.t = S.sb([128, words], F32)
        self.words = words
        self.off = 0

    def reset(self, off=0):
        self.off = off

    def f32(self, n):
        a = self.t[:, self.off:self.off + n]
        self.off += n
        assert self.off <= self.words, f"arena overflow {self.off}"
        return a

    def bf16(self, n):
        w = (n + 1) // 2
        a = self.t[:, self.off:self.off + w].bitcast(BF16)
        self.off += w
        assert self.off <= self.words, f"arena overflow {self.off}"
        return a


class Rot:
    def __init__(self, items):
        self.items = items
        self.i = 0

    def next(self):
        it = self.items[self.i % len(self.items)]
        self.i += 1
        return it


def build(NT, SEG, T, dbg=False, stop=None):
    nc = bass.Bass("TRN2", target_bir_lowering=False)
    S = Sched(nc)
    NTL = NT // T
    SUB = T // 128
    NCH = T // 64
    NKT = NT // 128

    def din(name, shape):
        return nc.dram_tensor(name, list(shape), F32, kind="ExternalInput").ap()

    def dscr(name, shape, dt):
        if dbg:
            return nc.dram_tensor(name, list(shape), dt, kind="ExternalOutput").ap()
        return nc.dram_tensor(name, list(shape), dt).ap()

    def finish():
        S.emit()
        S.close()
        return nc

    x = din("x", [NT, D])
    wgu = [din("wgu1", [D, 2 * DFF]), din("wgu2", [D, 2 * DFF])]
    wdn = [din("wd1", [DFF, D]), din("wd2", [DFF, D])]
    win = din("win", [D, 3456])
    wout = din("wout", [D, D])
    gvec = din("gvec", [128, 4, 8])
    taps = din("taps", [128, 3, 15])
    w0c = din("w0c", [128, 2, 4])
    a0c = din("a0c", [128, 2, 4])
    wup = din("wup", [128, 512])
    aup = din("aup", [128, 512])
    gup = din("gup", [128, 512])
    rwv = din("rwv", [128, 5, 4])
    gqk = din("gqk", [128, 2])
    gqkb = din("gqkb", [128, 2, 64])
    lamb = din("lamb", [128, 4, 64])
    sublnb = din("sublnb", [128, 128])
    cident = din("cident", [128, 128])
    cblk = din("cblk", [128, 128])
    cperm = din("cperm", [128, 128])
    cmask = din("cmask", [128, 2, 128])
    cmaskt = din("cmaskt", [128, 2, 64])
    cid64 = din("cid64", [128, 64])
    cscan = din("cscan", [128, 4 * T])
    costab = din("costab", [128, NT])
    sintab = din("sintab", [128, NT])
    flagin = din("flag", [128, 1])
    y = nc.dram_tensor("y", [NT, D], F32, kind="ExternalOutput").ap()

    wgu_t = [dscr("wgu1t", [NJ, 128, 8 * 256], BF16), dscr("wgu2t", [NJ, 128, 8 * 256], BF16)]
    wdn_t = [dscr("wd1t", [KC, 128, NJ * 128], BF16), dscr("wd2t", [KC, 128, NJ * 128], BF16)]
    win_bf = dscr("winb", [D, 3456], BF16)
    wout_bf = dscr("woutb", [D, D], BF16)
    wconv = dscr("wconv", [15, 128, 3 * 8 * 128], BF16)
    X1T = dscr("x1t", [8, 128, NT], F32)
    URW = dscr("urw", [15, 128, NT], F32)
    QT = dscr("qt", [4, 128, NT], BF16)
    KT = dscr("kt", [4, 128, NT], BF16)
    VDA = dscr("vda", [NT, 512], BF16)
    YRW = dscr("yrw", [2, 4, 128, NT], F32)
    YMIX = dscr("ymix", [8, 128, NT], BF16)

    PSD = [S.ps([128, 1024], F32) for _ in range(4)]
    PSB = [(PSD[i // 2][:, (i % 2) * 512:(i % 2 + 1) * 512], Buf()) for i in range(8)]
    psrot = Rot(PSB)

    CW = 6200
    cst = S.sb([128, CW], F32)
    coff = [0]

    def cf32(n):
        a = cst[:, coff[0]:coff[0] + n]
        coff[0] += n
        assert coff[0] <= CW
        return a

    def cbf(n):
        w = (n + 1) // 2
        a = cst[:, coff[0]:coff[0] + w].bitcast(BF16)
        coff[0] += w
        assert coff[0] <= CW
        return a

    cb = Buf()

    def ld(dst, src, eng='sp'):
        S.dma(eng, I('dma_start', out=dst, in_=src), cb, writes=[cb])

    identF = cf32(128); ld(identF, cident)
    blkF = cf32(128); ld(blkF, cblk)
    permF = cf32(128); ld(permF, cperm)
    maskF = cf32(256); ld(maskF, cmask.rearrange("p a b -> p (a b)"))
    masktF = cf32(128); ld(masktF, cmaskt.rearrange("p a b -> p (a b)"))
    id64F = cf32(64); ld(id64F, cid64)
    gv = cf32(32); ld(gv, gvec.rearrange("p a b -> p (a b)"))
    tp = cf32(45); ld(tp, taps.rearrange("p a b -> p (a b)"))
    w0t = cf32(8); ld(w0t, w0c.rearrange("p a b -> p (a b)"))
    a0t = cf32(8); ld(a0t, a0c.rearrange("p a b -> p (a b)"))
    rwt = cf32(20); ld(rwt, rwv.rearrange("p a b -> p (a b)"))
    gqt = cf32(2); ld(gqt, gqk)
    gqbt = cf32(128); ld(gqbt, gqkb.rearrange("p a b -> p (a b)"))
    lamt = cf32(256); ld(lamt, lamb.rearrange("p a b -> p (a b)"))
    subl = cf32(128); ld(subl, sublnb)
    flag = cf32(1); ld(flag, flagin)
    wupF = cf32(512); ld(wupF, wup)
    aupF = cf32(512); ld(aupF, aup)
    gupF = cf32(512); ld(gupF, gup)
    identB = cbf(128)
    blkB = cbf(128)
    blk64B = cbf(128)
    permB = cbf(128)
    onesMS = cbf(128)
    blk64F = cf32(128)
    id64B = cbf(4 * 64)
    mask4 = cf32(2 * 4 * 128)
    maskt4 = cf32(2 * 4 * 64)
    wupB = cbf(512)
    aupB = cbf(512)
    gupB = cbf(512)
    omka = cf32(4)
    eps6 = cf32(1)
    epsgn = cf32(1)
    bsame = cf32(1)
    bcross = cf32(1)
    nlam = cf32(1)
    sl8 = cf32(128)
    gq8 = cf32(1)
    tmpc = cf32(64)
    tmpc2 = cf32(4)

    def cop(eng, fn):
        S.op(eng, fn, reads=[cb], writes=[cb])

    cop('dve', I('tensor_copy', out=identB, in_=identF))
    cop('dve', I('tensor_copy', out=blkB, in_=blkF))
    cop('dve', I('tensor_scalar', out=blk64B, in0=blkF, scalar1=1.0 / 64, scalar2=None, op0=ALU.mult))
    cop('dve', I('tensor_scalar', out=blk64F, in0=blkF, scalar1=1.0 / 64, scalar2=None, op0=ALU.mult))
    cop('dve', I('tensor_copy', out=permB, in_=permF))
    cop('dve', I('memset', onesMS, 1.0 / 1024))
    for hp in range(4):
        cop('dve', I('tensor_copy', out=id64B[:, hp * 64:(hp + 1) * 64], in_=id64F))
        for d in range(2):
            cop('dve', I('tensor_copy',
                out=mask4[:, (d * 4 + hp) * 128:(d * 4 + hp + 1) * 128], in_=maskF[:, d * 128:(d + 1) * 128]))
            cop('dve', I('tensor_copy',
                out=maskt4[:, (d * 4 + hp) * 64:(d * 4 + hp + 1) * 64], in_=masktF[:, d * 64:(d + 1) * 64]))
    cop('dve', I('tensor_copy', out=wupB, in_=wupF))
    cop('dve', I('tensor_copy', out=aupB, in_=aupF))
    cop('dve', I('tensor_copy', out=gupB, in_=gupF))
    cop('dve', I('tensor_scalar', out=omka, in0=rwt[:, 4:8], scalar1=-1.0, scalar2=1.0, op0=ALU.mult, op1=ALU.add))
    cop('dve', I('memset', eps6, 1e-6))
    cop('dve', I('memset', epsgn, GN_EPS))
    cop('dve', I('tensor_reduce', out=tmpc2[:, 0:2], in_=gqbt.rearrange("p (a b) -> p a b", a=2),
                                         axis=AX.X, op=ALU.max, apply_absolute_value=True))
    cop('dve', I('tensor_tensor', out=tmpc2[:, 2:3], in0=tmpc2[:, 0:1], in1=tmpc2[:, 1:2], op=ALU.mult))
    cop('dve', I('tensor_scalar', out=bsame, in0=tmpc2[:, 2:3], scalar1=-8.0, scalar2=None, op0=ALU.mult))
    cop('dve', I('tensor_scalar', out=tmpc2[:, 3:4], in0=flag, scalar1=NEG, scalar2=-NEG, op0=ALU.mult, op1=ALU.add))
    cop('dve', I('tensor_tensor', out=bcross, in0=bsame, in1=tmpc2[:, 3:4], op=ALU.add))
    lam3 = lamt.rearrange("p (a b) -> p a b", a=4)
    cop('dve', I('tensor_tensor', out=tmpc, in0=lam3[:, 0, :], in1=lam3[:, 1, :], op=ALU.mult))
    cop('dve', I('tensor_reduce', out=tmpc2[:, 0:1], in_=tmpc, axis=AX.X, op=ALU.add))
    cop('dve', I('tensor_tensor', out=tmpc, in0=lam3[:, 2, :], in1=lam3[:, 3, :], op=ALU.mult))
    cop('dve', I('tensor_reduce', out=tmpc2[:, 1:2], in_=tmpc, axis=AX.X, op=ALU.add))
    cop('act', I('activation', out=tmpc2[:, 0:2], in_=tmpc2[:, 0:2], func=AF.Exp))
    cop('dve', I('tensor_tensor', out=tmpc2[:, 2:3], in0=tmpc2[:, 1:2], in1=tmpc2[:, 0:1], op=ALU.subtract))
    cop('dve', I('tensor_scalar', out=nlam, in0=tmpc2[:, 2:3], scalar1=-0.2, scalar2=None, op0=ALU.add))
    cop('dve', I('tensor_scalar', out=sl8, in0=subl, scalar1=0.8, scalar2=None, op0=ALU.mult))
    cop('dve', I('tensor_scalar', out=gq8, in0=gqt[:, 0:1], scalar1=0.125, scalar2=None, op0=ALU.mult))

    wcbs = Rot([Buf() for _ in range(6)])

    def castw(dst, src, rows):
        for r0 in range(0, rows, 128):
            wcb = wcbs.next()
            S.dma('pool', I('dma_start', out=dst[r0:r0 + 128, :], in_=src[r0:r0 + 128, :]), wcb, writes=[wcb])

    import os
    NOCAST = os.environ.get("NOCAST")
    for i in range(2):
        wgr_ = wgu[i].rearrange("(k p) n -> p k n", p=128)
        for j in range(NJ):
            for g in range(2):
                wcb = wcbs.next()
                S.dma('pool', I('dma_start', out=wgu_t[i][j].rearrange("p (k g c) -> p k g c", k=8, g=2)[:, :, g, :],
                                in_=wgr_[:, :, g * DFF + j * 128: g * DFF + (j + 1) * 128]), wcb, writes=[wcb])
        wdr_ = wdn[i].rearrange("(j p) n -> p j n", p=128)
        for n in range(KC):
            wcb = wcbs.next()
            S.dma('pool', I('dma_start', out=wdn_t[i][n].rearrange("p (j c) -> p j c", j=NJ), in_=wdr_[:, :, n * 128:(n + 1) * 128]),
                  wcb, writes=[wcb])
    castw(win_bf, win, D)
    castw(wout_bf, wout, D)

    AR_WORDS = 46900
    ar = Arena(S, AR_WORDS)

    tapsrow = din("tapsrow", [128, 3, 1920])
    ar.reset()
    trow = ar.f32(3 * 1920); btr = Buf()
    S.dma('sp', I('dma_start', out=trow, in_=tapsrow.rearrange("p a b -> p (a b)")), btr, writes=[btr])
    wl = [(ar.f32(8 * 128), Buf()) for _ in range(2)]
    wo_ = [(ar.bf16(3 * 8 * 128), Buf()) for _ in range(2)]
    winr = win.rearrange("(kc p) n -> p kc n", p=128)
    for m in range(15):
        wt, wb_ = wl[m % 2]
        ot, ob = wo_[m % 2]
        wt3 = wt.rearrange("p (k c) -> p k c", k=8)
        S.dma('sp', I('dma_start', out=wt3, in_=winr[:, :, m * 128:(m + 1) * 128]), wb_, writes=[wb_])
        for d in range(3):
            o3 = ot[:, d * 1024:(d + 1) * 1024].rearrange("p (k c) -> p k c", k=8)
            tb = trow[:, d * 1920 + m * 128: d * 1920 + (m + 1) * 128]
            S.op('dve', I('tensor_tensor',
                out=o3, in0=wt3, in1=tb.unsqueeze(1).to_broadcast([128, 8, 128]), op=ALU.mult),
                reads=[wb_, btr], writes=[ob])
        S.dma('sp', I('dma_start', out=wconv[m], in_=ot), ob, reads=[ob])
    S.barrier()

    if stop == 'pro':
        return finish()

    def evac_copy(eng, dst, src, reads, writes):
        if eng == 'act':
            S.op('act', I('activation', out=dst, in_=src, func=AF.Copy), reads=reads, writes=writes)
        else:
            S.op(eng, I('tensor_copy', out=dst, in_=src), reads=reads, writes=writes)

    def rstd_from_ps(ps_ap, rdst, epst, reads, wb, width):
        S.op('act', I('activation', out=rdst, in_=ps_ap, func=AF.Ln, bias=epst[:, 0:1], scale=1.0),
             reads=reads + [cb], writes=[wb])
        S.op('act', I('activation', out=rdst, in_=rdst, func=AF.Exp, scale=-0.5), reads=[wb], writes=[wb])

    def rmsnorm_g(xT3, xbufs, which, out3, obuf, sq3, sqb, rst, rsb):
        for half in range(2):
            S.op('act', I('activation', out=sq3[:, half * 4:(half + 1) * 4, :], in_=xT3[:, half * 4:(half + 1) * 4, :], func=AF.Square),
                 reads=xbufs, writes=[sqb])
            yield
        ps, pb = psrot.next()
        for kc in range(KC):
            S.op('pe', I('matmul', ps[:, 0:T], lhsT=onesMS, rhs=sq3[:, kc, :], start=(kc == 0), stop=(kc == KC - 1)),
                 reads=[sqb, cb], writes=[pb])
        S.op('act', I('activation', out=rst, in_=ps[:, 0:T], func=AF.Ln, bias=eps6[:, 0:1], scale=1.0), reads=[pb, cb], writes=[rsb])
        yield
        S.op('act', I('activation', out=rst, in_=rst, func=AF.Exp, scale=-0.5), reads=[rsb], writes=[rsb])
        yield
        for kc in range(KC):
            S.op('dve', I('scalar_tensor_tensor', out=out3[:, kc, :], in0=xT3[:, kc, :], scalar=gv[:, which * 8 + kc: which * 8 + kc + 1], in1=rst,
                 op0=ALU.mult, op1=ALU.mult), reads=[xbufs[kc], rsb, cb], writes=[obuf])
            if kc % 2 == 1:
                yield

    def rmsnorm(*a_, **k_):
        for _ in rmsnorm_g(*a_, **k_):
            pass

    def ffn_g(xT3, xbufs, which_norm, li, R):
        hT3, hb = R['hT']
        yield from rmsnorm_g(xT3, xbufs, which_norm, hT3, hb, R['sq'][0], R['sq'][1], R['rst'][0], R['rst'][1])
        actT3, actb = R['actT']
        for j in range(NJ):
            wt, wb_ = R['wgu'].next()
            wt3 = wt.rearrange("p (k c) -> p k c", k=8)
            S.dma('sp', I('dma_start', out=wt, in_=wgu_t[li][j]), wb_, writes=[wb_])
            pg, pgb = psrot.next()
            pu, pub = psrot.next()
            for kc in range(KC):
                S.op('pe', I('matmul', pg[:, 0:T], lhsT=wt3[:, kc, 0:128], rhs=hT3[:, kc, :],
                                                                     start=(kc == 0), stop=(kc == KC - 1)),
                     reads=[wb_, hb], writes=[pgb])
            for kc in range(KC):
                S.op('pe', I('matmul', pu[:, 0:T], lhsT=wt3[:, kc, 128:256], rhs=hT3[:, kc, :],
                                                                     start=(kc == 0), stop=(kc == KC - 1)),
                     reads=[wb_, hb], writes=[pub])
            sg, sgb = R['sg'].next()
            S.op('act', I('activation', out=sg, in_=pg[:, 0:T], func=AF.Silu), reads=[pgb], writes=[sgb])
            S.op('dve', I('tensor_tensor', out=actT3[:, j, :], in0=sg, in1=pu[:, 0:T], op=ALU.mult),
                 reads=[sgb, pub], writes=[actb[j]])
            yield
        for n in range(KC):
            wt, wb_ = R['wdn'].next()
            wt3 = wt.rearrange("p (j c) -> p j c", j=NJ)
            S.dma('sp', I('dma_start', out=wt, in_=wdn_t[li][n]), wb_, writes=[wb_])
            pd, pdb = psrot.next()
            for j in range(NJ):
                S.op('pe', I('matmul', pd[:, 0:T], lhsT=wt3[:, j, :], rhs=actT3[:, j, :],
                                                                   start=(j == 0), stop=(j == NJ - 1)),
                     reads=[wb_, actb[j]], writes=[pdb])
            S.op('dve', I('scalar_tensor_tensor', out=xT3[:, n, :], in0=pd[:, 0:T], scalar=0.5, in1=xT3[:, n, :],
                                                                      op0=ALU.mult, op1=ALU.add),
                 reads=[pdb, xbufs[n]], writes=[xbufs[n]])
            yield

    def ffn(*a_, **k_):
        for _ in ffn_g(*a_, **k_):
            pass

    def ffn_resources(ngu=2, ndn=2):
        R = {}
        R['hT'] = (ar.bf16(8 * T).rearrange("p (k t) -> p k t", k=8), Buf())
        R['sq'] = (ar.bf16(8 * T).rearrange("p (k t) -> p k t", k=8), Buf())
        R['rst'] = (ar.f32(T), Buf())
        R['actT'] = (ar.bf16(NJ * T).rearrange("p (j t) -> p j t", j=NJ), [Buf() for _ in range(NJ)])
        R['wgu'] = Rot([(ar.bf16(8 * 256), Buf()) for _ in range(ngu)])
        R['wdn'] = Rot([(ar.bf16(NJ * 128), Buf()) for _ in range(ndn)])
        R['sg'] = Rot([(ar.f32(T), Buf()) for _ in range(2)])
        return R

    ar.reset()
    R = ffn_resources()
    xT3 = ar.f32(8 * T).rearrange("p (k t) -> p k t", k=8)
    xbufs = [Buf() for _ in range(KC)]
    xin = [(ar.f32(SUB * D).rearrange("p (s d) -> p s d", s=SUB), Buf()) for _ in range(1)]
    TE = T + 2
    h2e = [(ar.bf16(8 * TE).rearrange("p (k t) -> p k t", k=8), Buf()) for _ in range(4)]
    wcs = Rot([(ar.bf16(3 * 8 * 128), Buf()) for _ in range(2)])
    wv = (ar.bf16(8 * 512).rearrange("p (k c) -> p k c", k=8), Buf())
    ut = Rot([(ar.f32(T), Buf()) for _ in range(2)])
    cs = Rot([(ar.f32(2 * T), Buf()) for _ in range(1)])
    vo = Rot([(ar.bf16(512), Buf()) for _ in range(2)])

    winbr = win_bf.rearrange("(kc p) n -> p kc n", p=128)
    S.dma('sp', I('dma_start', out=wv[0], in_=winbr[:, :, 1920 + 1024: 3456]), wv[1], writes=[wv[1]])

    def qk_bufs():
        return dict(t1=(ar.f32(T), Buf()), t2=(ar.f32(T), Buf()), t3=(ar.f32(T), Buf()),
                    sq=(ar.bf16(T), Buf()), qnb=(ar.bf16(T), Buf()), qo=Rot([(ar.bf16(T), Buf()) for _ in range(2)]),
                    wqk=Rot([(ar.bf16(8 * 128).rearrange("p (k c) -> p k c", k=8), Buf()) for _ in range(1)]))

    QKB = [qk_bufs(), qk_bufs()]

    def proj_rw(i, he, heb):
        t0 = i * T
        for m in range(15):
            wt, wb_ = wcs.next()
            S.dma('sp', I('dma_start', out=wt, in_=wconv[m]), wb_, writes=[wb_])
            wt4 = wt.rearrange("p (d k c) -> p d k c", d=3, k=8)
            ps, pb = psrot.next()
            n = 0
            for d in range(3):
                for kc in range(KC):
                    S.op('pe', I('matmul', ps[:, 0:T], lhsT=wt4[:, d, kc, :], rhs=he[:, kc, d:d + T], start=(n == 0), stop=(n == 23)),
                         reads=[wb_, heb], writes=[pb])
                    n += 1
            u, ub = ut.next()
            evac_copy('act', u, ps[:, 0:T], [pb], [ub])
            S.dma('sp', I('dma_start', out=URW[m, :, t0:t0 + T], in_=u), ub, reads=[ub])
            yield
        for s_ in range(SUB):
            ps, pb = psrot.next()
            for kc in range(KC):
                S.op('pe', I('matmul', ps[:, 0:512], lhsT=he[:, kc, 1 + s_ * 128: 1 + (s_ + 1) * 128], rhs=wv[0][:, kc, :],
                             start=(kc == 0), stop=(kc == KC - 1)), reads=[wv[1], heb], writes=[pb])
            v_, vb_ = vo.next()
            evac_copy('act', v_, ps[:, 0:512], [pb], [vb_])
            S.dma('sp', I('dma_start', out=VDA[t0 + s_ * 128: t0 + (s_ + 1) * 128, :], in_=v_), vb_, reads=[vb_])
            yield

    def proj_qk(i, he, heb, ms, QB_, ct, ctb):
        t0 = i * T
        r_, rb_ = QB_['t1']; qn, qnbuf = QB_['t2']; tt, ttb = QB_['t3']
        sq, sqbb = QB_['sq']; qb_, qbb = QB_['qnb']
        for m in ms:
            isq = m < 4
            wq_, wqb_ = QB_['wqk'].next()
            S.dma('sp', I('dma_start', out=wq_, in_=winbr[:, :, 1920 + m * 128: 1920 + (m + 1) * 128]), wqb_, writes=[wqb_])
            ps, pb = psrot.next()
            for kc in range(KC):
                S.op('pe', I('matmul', ps[:, 0:T], lhsT=wq_[:, kc, :], rhs=he[:, kc, 1:1 + T],
                             start=(kc == 0), stop=(kc == KC - 1)), reads=[wqb_, heb], writes=[pb])
            S.op('act', I('activation', out=tt, in_=ps[:, 0:T], func=AF.Copy), reads=[pb], writes=[ttb])
            yield
            S.op('act', I('activation', out=sq, in_=tt, func=AF.Square), reads=[ttb], writes=[sqbb])
            yield
            ps2, pb2 = psrot.next()
            S.op('pe', I('matmul', ps2[:, 0:T], lhsT=blk64B, rhs=sq, start=True, stop=True), reads=[sqbb, cb], writes=[pb2])
            S.op('act', I('activation', out=r_, in_=ps2[:, 0:T], func=AF.Ln, bias=eps6[:, 0:1], scale=1.0), reads=[pb2, cb], writes=[rb_])
            yield
            S.op('act', I('activation', out=r_, in_=r_, func=AF.Exp, scale=-0.5), reads=[rb_], writes=[rb_])
            yield
            gsc = gq8 if isq else gqt[:, 1:2]
            S.op('dve', I('scalar_tensor_tensor', out=qn, in0=tt, scalar=gsc, in1=r_, op0=ALU.mult, op1=ALU.mult),
                 reads=[ttb, rb_, cb], writes=[qnbuf])
            yield
            S.op('pool', I('tensor_copy', out=qb_, in_=qn), reads=[qnbuf], writes=[qbb])
            yield
            ps3, pb3 = psrot.next()
            S.op('pe', I('matmul', ps3[:, 0:T], lhsT=permB, rhs=qb_, start=True, stop=True), reads=[qbb, cb], writes=[pb3])
            S.op('dve', I('tensor_tensor', out=tt, in0=ps3[:, 0:T], in1=ct[:, T:2 * T], op=ALU.mult), reads=[pb3, ctb, qnbuf], writes=[ttb])
            S.op('pool', I('tensor_tensor', out=qn, in0=qn, in1=ct[:, 0:T], op=ALU.mult), reads=[qnbuf, ctb], writes=[qnbuf])
            yield
            o_, ob_ = QB_['qo'].next()
            S.op('dve', I('tensor_tensor', out=o_, in0=qn, in1=tt, op=ALU.add), reads=[qnbuf, ttb], writes=[ob_])
            dst = QT if isq else KT
            S.dma('sp', I('dma_start', out=dst[m % 4, :, t0:t0 + T], in_=o_), ob_, reads=[ob_])
            yield

    def run_threads(gens):
        gens = list(gens)
        while gens:
            for g in list(gens):
                try:
                    next(g)
                except StopIteration:
                    gens.remove(g)

    def project_threads(i):
        t0 = i * T
        he, heb = h2e[i % len(h2e)]
        ct, ctb = cs.next()
        S.dma('sp', [I('dma_start', out=ct[:, 0:T], in_=costab[:, t0:t0 + T]),
                     I('dma_start', out=ct[:, T:2 * T], in_=sintab[:, t0:t0 + T])], ctb, writes=[ctb])
        return [proj_rw(i, he, heb), proj_qk(i, he, heb, [0, 4, 1, 5], QKB[0], ct, ctb), proj_qk(i, he, heb, [2, 6, 3, 7], QKB[1], ct, ctb)]

    NH = len(h2e)

    def tile_ffn(i):
        t0 = i * T
        xi, xib = xin[0]
        S.dma('sp', I('dma_start', out=xi, in_=x[t0:t0 + T, :].rearrange("(s p) d -> p s d", p=128)), xib, writes=[xib])
        for kc in range(KC):
            ps, pb = psrot.next()
            for s_ in range(SUB):
                S.op('pe', I('transpose', out=ps[:, s_ * 128:(s_ + 1) * 128], in_=xi[:, s_, kc * 128:(kc + 1) * 128], identity=identF),
                     reads=[xib, cb], writes=[pb])
            evac_copy('act' if kc % 2 else 'dve', xT3[:, kc, :], ps[:, 0:T], [pb], [xbufs[kc]])
            yield
        yield from ffn_g(xT3, xbufs, 0, 0, R)
        S.dma('sp', I('dma_start', out=X1T.rearrange("k p t -> p k t")[:, :, t0:t0 + T], in_=xT3), xbufs[0], reads=xbufs)
        he, heb = h2e[i % NH]
        yield from rmsnorm_g(xT3, xbufs, 1, he[:, :, 1:1 + T], heb, R['sq'][0], R['sq'][1], R['rst'][0], R['rst'][1])
        if i == 0:
            S.op('dve', I('memset', he[:, :, 0:1], 0.0), writes=[heb])
        else:
            hp_, hpb = h2e[(i - 1) % NH]
            if t0 % SEG == 0:
                S.op('dve', I('tensor_scalar', out=he[:, :, 0:1], in0=hp_[:, :, T:T + 1], scalar1=flag[:, 0:1], scalar2=None, op0=ALU.mult),
                     reads=[hpb, cb], writes=[heb])
                S.op('dve', I('tensor_scalar', out=hp_[:, :, T + 1:T + 2], in0=he[:, :, 1:2], scalar1=flag[:, 0:1], scalar2=None, op0=ALU.mult),
                     reads=[heb, cb], writes=[hpb])
            else:
                S.op('dve', I('tensor_copy', out=he[:, :, 0:1], in_=hp_[:, :, T:T + 1]), reads=[hpb], writes=[heb])
                S.op('dve', I('tensor_copy', out=hp_[:, :, T + 1:T + 2], in_=he[:, :, 1:2]), reads=[heb], writes=[hpb])
        if i == NTL - 1:
            S.op('dve', I('memset', he[:, :, T + 1:T + 2], 0.0), writes=[heb])
        yield

    for i in range(NTL + 2):
        threads = []
        if i < NTL:
            threads.append(tile_ffn(i))
        if 0 <= i - 2 < NTL:
            threads += project_threads(i - 2)
        run_threads(threads)
    S.barrier()

    if stop == 'A':
        return finish()
    ar.reset()
    KTs = ar.bf16(4 * NT).rearrange("p (h t) -> p h t", h=4); ktb = Buf()
    VA = ar.bf16(NKT * 4 * 130).rearrange("p (k h e) -> p k h e", k=NKT, h=4); vab = Buf()
    TQ = min(512, SEG, NT)
    NQT = NT // TQ
    QB = TQ // 128
    Qs = [(ar.bf16(4 * TQ).rearrange("p (h t) -> p h t", h=4), Buf()) for _ in range(2)]
    Es = Rot([(ar.bf16(2 * TQ), Buf()) for _ in range(3)])
    of = Rot([(ar.f32(128), Buf()) for _ in range(2)])
    on = Rot([(ar.f32(128), Buf()) for _ in range(2)])
    sm = Rot([(ar.f32(8), Buf()) for _ in range(2)])
    junk = (ar.f32(128), Buf())
    yo = Rot([(ar.bf16(TQ), Buf()) for _ in range(2)])
    for h in range(4):
        S.dma('sp', I('dma_start', out=KTs[:, h, :], in_=KT[h]), ktb, writes=[ktb])
    vsrc = VDA.rearrange("(k p) (h e) -> p k h e", p=128, h=4)
    for h in range(4):
        S.dma('sp', I('dma_start', out=VA[:, :, h, 0:128], in_=vsrc[:, :, h, :]), vab, writes=[vab])
    S.op('pool', I('memset', VA[:, :, :, 128:129], 1.0), writes=[vab])
    PSP = Rot([(PSD[0], PSB[0][1], PSB[1][1]), (PSD[1], PSB[2][1], PSB[3][1])])
    OB = [[(PSB[4 + (c * 4 + qb) // 2][0][:, ((c * 4 + qb) % 2) * 256: ((c * 4 + qb) % 2) * 256 + 129], PSB[4 + (c * 4 + qb) // 2][1])
           for qb in range(4)] for c in range(2)]
    osb = Rot([(ar.f32(8 * 132), Buf()) for _ in range(2)])
    LOOK = 1

    def finalize(h, q0):
        os_, osb_ = osb.next()
        o4 = os_.rearrange("p (c q e) -> p c q e", c=2, q=4)
        for c in range(2):
            for qb in range(QB):
                o_ap, o_b = OB[c][qb]
                S.op('dve', I('tensor_copy', out=o4[:, c, qb, 0:129], in_=o_ap), reads=[o_b], writes=[osb_])
        pd_, pstb, _pb1 = PSP.next()
        pst = pd_[:, 0:512]
        for qb in range(QB):
            o1 = o4[:, 0, qb, :]
            o2 = o4[:, 1, qb, :]
            s_, sb_ = sm.next()
            S.op('dve', I('reciprocal', out=s_[:, 0:1], in_=o1[:, 128:129]), reads=[osb_], writes=[sb_])
            S.op('dve', I('reciprocal', out=s_[:, 1:2], in_=o2[:, 128:129]), reads=[osb_], writes=[sb_])
            S.op('dve', I('tensor_tensor', out=s_[:, 1:2], in0=s_[:, 1:2], in1=nlam, op=ALU.mult), reads=[sb_, cb], writes=[sb_])
            f_, fb_ = of.next()
            S.op('dve', I('tensor_scalar', out=f_, in0=o1[:, 0:128], scalar1=s_[:, 0:1], scalar2=None, op0=ALU.mult),
                 reads=[osb_, sb_], writes=[fb_])
            S.op('dve', I('scalar_tensor_tensor', out=f_, in0=o2[:, 0:128], scalar=s_[:, 1:2], in1=f_, op0=ALU.mult, op1=ALU.add),
                 reads=[osb_, sb_, fb_], writes=[fb_])
            S.op('pool', I('tensor_tensor', out=junk[0], in0=f_, in1=f_, op=ALU.mult), reads=[fb_], writes=[junk[1]])
            S.op('dve', I('tensor_reduce', out=s_[:, 2:3], in_=junk[0], axis=AX.X, op=ALU.add), reads=[junk[1]], writes=[sb_])
            S.op('act', I('activation', out=s_[:, 3:4], in_=s_[:, 2:3], func=AF.Ln, bias=eps6[:, 0:1], scale=1.0 / 128),
                 reads=[sb_, cb], writes=[sb_])
            S.op('act', I('activation', out=s_[:, 3:4], in_=s_[:, 3:4], func=AF.Exp, scale=-0.5), reads=[sb_], writes=[sb_])
            n_, nb_ = on.next()
            S.op('dve', I('scalar_tensor_tensor', out=n_, in0=f_, scalar=s_[:, 3:4], in1=sl8, op0=ALU.mult, op1=ALU.mult),
                 reads=[fb_, sb_, cb], writes=[nb_])
            S.op('pe', I('transpose', out=pst[:, qb * 128:(qb + 1) * 128], in_=n_, identity=identF), reads=[nb_, cb], writes=[pstb])
        yt, ytb = yo.next()
        evac_copy('dve', yt, pst[:, 0:TQ], [pstb], [ytb])
        S.dma('sp', I('dma_start', out=YMIX[4 + h, :, q0:q0 + TQ], in_=yt), ytb, reads=[ytb])

    for qt in range(NQT):
        q0 = qt * TQ
        Qt, Qb = Qs[qt % 2]
        S.dma('sp', I('dma_start', out=Qt, in_=QT.rearrange("h p t -> p h t")[:, :, q0:q0 + TQ]), Qb, writes=[Qb])
        seq = [(h, kt) for h in range(4) for kt in range(NKT)]
        pend = {}
        for idx in range(len(seq) + LOOK):
            if idx < len(seq):
                h, kt = seq[idx]
                pd_, pb0, pb1 = PSP.next()
                p2 = pd_.rearrange("p (c n) -> p c n", c=2)
                for c in range(2):
                    S.op('pe', I('matmul', p2[:, c, 0:TQ], lhsT=KTs[c * 64:(c + 1) * 64, h, kt * 128:(kt + 1) * 128], rhs=Qt[c * 64:(c + 1) * 64, h, :],
                                 start=True, stop=True, tile_position=(c * 64, 0)), reads=[ktb, Qb], writes=[pb0 if c == 0 else pb1])
                same = (kt * 128) // SEG == q0 // SEG
                bias = bsame if same else bcross
                E, Eb = Es.next()
                E3 = E.rearrange("p (c n) -> p c n", c=2)
                S.op('act', I('activation', out=E3, in_=p2[:, :, 0:TQ], func=AF.Exp, bias=bias[:, 0:1], scale=1.0), reads=[pb0, pb1, cb], writes=[Eb])
                pend[idx] = (E3, Eb)
            j = idx - LOOK
            if j >= 0:
                h, kt = seq[j]
                E3, Eb = pend.pop(j)
                for c in range(2):
                    for qb in range(QB):
                        o_ap, o_b = OB[c][qb]
                        S.op('pe', I('matmul', o_ap, lhsT=E3[:, c, qb * 128:(qb + 1) * 128], rhs=VA[:, kt, h, 0:129],
                                     start=(kt == 0 and qb % 2 == 0), stop=(kt == NKT - 1 and (qb % 2 == 1 or qb == QB - 1))),
                             reads=[Eb, vab], writes=[o_b])
                if kt == NKT - 1:
                    finalize(h, q0)
    S.barrier()

    if stop == 'B1':
        return finish()
    ar.reset()
    TB = min(T, 256)
    NTB = NT // TB
    NCB = TB // 64
    T4 = 4 * TB

    def v3(ap):
        return ap.rearrange("p (h t) -> p h t", h=4)

    def v4(ap):
        return ap.rearrange("p (h c s) -> p h c s", h=4, c=NCB)

    def vc(ap):
        return ap.rearrange("p (g s) -> p g s", s=64)

    def hv(ap):
        return ap.rearrange("p (h t) -> p h t", h=4)

    def hsl(hs):
        return slice(hs * 64, (hs + 1) * 64)

    def headmm(out_fn, l_fn, r_fn, reads, writes, more=()):
        terms = [(l_fn, r_fn, reads)] + list(more)
        for h in range(8):
            hp_, hs = h // 2, h % 2
            for ti, (lf, rf, rd) in enumerate(terms):
                S.op('pe', I('matmul', out_fn(hp_, hs), lhsT=lf(hp_, hs), rhs=rf(hp_, hs), start=(ti == 0), stop=(ti == len(terms) - 1),
                             tile_position=(hs * 64, hs * 64)), reads=rd, writes=writes)

    def tb():
        return (ar.f32(T4), Buf())

    def tbh():
        return (ar.bf16(T4), Buf())

    scanm = tbh()
    scanf = (ar.f32(T4), Buf())
    S.dma('sp', I('dma_start', out=scanf[0], in_=cscan[:, 0:T4]), scanf[1], writes=[scanf[1]])
    S.op('dve', I('tensor_copy', out=scanm[0], in_=scanf[0]), reads=[scanf[1]], writes=[scanm[1]])
    shared = dict(SQ=tbh(), XB=tbh(), RIN=scanf)

    def lora_sig(dst, upB, inb, biasc, d):
        for hp_ in range(4):
            ps, pb = psrot.next()
            S.op('pe', I('matmul', ps[:, 0:TB], lhsT=upB[d * 64:(d + 1) * 64, hp_ * 128:(hp_ + 1) * 128],
                         rhs=inb[0][d * 64:(d + 1) * 64, :], start=True, stop=True, tile_position=(d * 64, 0)),
                 reads=[inb[1], cb], writes=[pb])
            S.op('act', I('activation', out=v3(dst[0])[:, hp_, :], in_=ps[:, 0:TB], func=AF.Sigmoid,
                          bias=biasc[:, d * 4 + hp_: d * 4 + hp_ + 1], scale=1.0), reads=[pb, cb], writes=[dst[1]])

    def kdir(dst, a_t, Kf, TMPa):
        for hp_ in range(4):
            S.op('dve', I('tensor_scalar', out=v3(TMPa[0])[:, hp_, :], in0=v3(a_t[0])[:, hp_, :],
                          scalar1=rwt[:, 4 + hp_:5 + hp_], scalar2=omka[:, hp_:hp_ + 1], op0=ALU.mult, op1=ALU.add),
                 reads=[a_t[1], cb], writes=[TMPa[1]])
        S.op('dve', I('tensor_tensor', out=dst[0], in0=Kf[0], in1=TMPa[0], op=ALU.mult), reads=[Kf[1], TMPa[1]], writes=[dst[1]])

    urw_r = URW.rearrange("m p t -> p m t")

    def alloc_dir():
        B = {}
        for nm_ in ('Rf', 'Kf', 'Vf', 'LW', 'Aa', 'KK', 'KD', 'Bv', 'CUM', 'E1'):
            B[nm_] = tb()
        B['EXC'] = B['Aa']
        B['E2'] = B['Kf']
        B['WDf'] = (ar.f32(TB), Buf()); B['ADf'] = (ar.f32(TB), Buf())
        B['twb'] = (ar.bf16(TB), Buf()); B['adb'] = (ar.bf16(TB), Buf())
        B['TOT'] = (ar.f32(4 * NCB), Buf()); B['EW'] = (ar.f32(4 * NCB), Buf())
        B['ARt'] = (ar.bf16(4 * NCB * 128), Buf()); B['BKt'] = (ar.bf16(4 * NCB * 128), Buf())
        B['VTt'] = tbh(); B['BHt'] = tbh(); B['KHt'] = tbh()
        B['Hf'] = (ar.f32(256), Buf()); B['Hb'] = (ar.bf16(256), Buf()); B['Ht'] = (ar.f32(256), Buf())
        B['SCB'] = [(ar.bf16(512), Buf()) for _ in range(NCB)]
        B['SCK'] = [(ar.bf16(512), Buf()) for _ in range(NCB)]
        B['PP'] = [Rot([(ar.bf16(256), Buf()) for _ in range(2)]) for _ in range(NCB)]
        B['PT'] = [Rot([(ar.bf16(256), Buf()) for _ in range(2)]) for _ in range(NCB)]
        B['TT'] = [Rot([(ar.bf16(256), Buf()) for _ in range(2)]) for _ in range(NCB)]
        B['Zb'] = Rot([(ar.bf16(256), Buf()) for _ in range(2)])
        B['Ub'] = Rot([(ar.bf16(256), Buf()) for _ in range(2)])
        B['Yc'] = Rot([(ar.f32(256), Buf()) for _ in range(2)])
        return B

    def to_tm(src, dstt):
        xb, xbb = shared['XB']
        S.op('pool', I('tensor_copy', out=xb, in_=src[0]), reads=[src[1]], writes=[xbb])
        x4 = v4(xb)
        d4 = v4(dstt[0])
        for hp_ in range(4):
            ps, pb = psrot.next()
            p3 = ps[:, 0:NCB * 64].rearrange("p (c s) -> p c s", c=NCB)
            for c in range(NCB):
                for hs in range(2):
                    S.op('pe', I('matmul', p3[hsl(hs), c, :], lhsT=x4[hsl(hs), hp_, c, :], rhs=identB[hsl(hs), hsl(hs)], start=True, stop=True,
                                 tile_position=(hs * 64, hs * 64)), reads=[xbb, cb], writes=[pb])
            evac_copy('act' if hp_ % 2 else 'dve', d4[:, hp_, :, :], p3, [pb], [dstt[1]])

    def rw_dir(dirn, B):
        Rf, Kf, Vf, LW, Aa, KK, KD, Bv, CUM, EXC, E1, E2 = (B[k] for k in ('Rf', 'Kf', 'Vf', 'LW', 'Aa', 'KK', 'KD', 'Bv', 'CUM', 'EXC', 'E1', 'E2'))
        WDf, ADf, twb, adb, TOT, EW, ARt, BKt, VTt, BHt, KHt, Hf, Hb, Ht = (B[k] for k in (
            'WDf', 'ADf', 'twb', 'adb', 'TOT', 'EW', 'ARt', 'BKt', 'VTt', 'BHt', 'KHt', 'Hf', 'Hb', 'Ht'))
        SQ = shared['SQ']; RIN = shared['RIN']; DC = LW
        S.op('dve', I('memset', Hf[0], 0.0), writes=[Hf[1]])
        S.op('dve', I('memset', Hb[0], 0.0), writes=[Hb[1]])
        yr = YRW[dirn].rearrange("m p t -> p m t")
        tiles = list(range(NTB)) if dirn == 0 else list(range(NTB - 1, -1, -1))
        for i in tiles:
            t0 = i * TB
            S.dma('sp', I('dma_start', out=v3(Rf[0]), in_=urw_r[:, 0:4, t0:t0 + TB]), Rf[1], writes=[Rf[1]])
            S.dma('sp', I('dma_start', out=v3(Kf[0]), in_=urw_r[:, 4:8, t0:t0 + TB]), Kf[1], writes=[Kf[1]])
            S.dma('sp', I('dma_start', out=v3(Vf[0]), in_=urw_r[:, 8:12, t0:t0 + TB]), Vf[1], writes=[Vf[1]])
            S.dma('sp', I('dma_start', out=WDf[0], in_=URW[12, :, t0:t0 + TB]), WDf[1], writes=[WDf[1]])
            S.dma('sp', I('dma_start', out=ADf[0], in_=URW[13, :, t0:t0 + TB]), ADf[1], writes=[ADf[1]])
            yield
            S.op('act', I('activation', out=twb[0], in_=WDf[0], func=AF.Tanh), reads=[WDf[1]], writes=[twb[1]])
            S.op('pool', I('tensor_copy', out=adb[0], in_=ADf[0]), reads=[ADf[1]], writes=[adb[1]])
            yield
            lora_sig(LW, wupB, twb, w0t, dirn)
            yield
            S.op('pool', I('tensor_scalar', out=LW[0], in0=LW[0], scalar1=-DECAY, scalar2=None, op0=ALU.mult), reads=[LW[1]], writes=[LW[1]])
            lora_sig(Aa, aupB, adb, a0t, dirn)
            yield
            for hp_ in range(4):
                S.op('dve', I('tensor_scalar', out=v3(KK[0])[:, hp_, :], in0=v3(Kf[0])[:, hp_, :], scalar1=rwt[:, hp_:hp_ + 1],
                              scalar2=None, op0=ALU.mult), reads=[Kf[1], cb], writes=[KK[1]])
            yield
            S.op('act', I('activation', out=SQ[0], in_=KK[0], func=AF.Square), reads=[KK[1]], writes=[SQ[1]])
            for hp_ in range(4):
                ps, pb = psrot.next()
                S.op('pe', I('matmul', ps[:, 0:TB], lhsT=blkB, rhs=v3(SQ[0])[:, hp_, :], start=True, stop=True), reads=[SQ[1], cb], writes=[pb])
                S.op('dve', I('tensor_scalar', out=v3(RIN[0])[:, hp_, :], in0=ps[:, 0:TB], scalar1=1e-24, scalar2=None, op0=ALU.max),
                     reads=[pb], writes=[RIN[1]])
            S.op('act', I('activation', out=RIN[0], in_=RIN[0], func=AF.Ln), reads=[RIN[1]], writes=[RIN[1]])
            S.op('act', I('activation', out=RIN[0], in_=RIN[0], func=AF.Exp, scale=-0.5), reads=[RIN[1]], writes=[RIN[1]])
            S.op('dve', I('tensor_tensor', out=KK[0], in0=KK[0], in1=RIN[0], op=ALU.mult), reads=[KK[1], RIN[1]], writes=[KK[1]])
            kdir(KD, Aa, Kf, RIN)
            yield
            S.op('pool', I('tensor_tensor', out=Bv[0], in0=KK[0], in1=Aa[0], op=ALU.mult), reads=[KK[1], Aa[1]], writes=[Bv[1]])
            S.op('dve', I('tensor_tensor_scan', out=CUM[0], data0=scanm[0], data1=LW[0], initial=0.0, op0=ALU.mult, op1=ALU.add),
                 reads=[scanm[1], LW[1]], writes=[CUM[1]])
            yield
            S.op('dve', I('tensor_copy', out=TOT[0], in_=vc(CUM[0])[:, :, 63]), reads=[CUM[1]], writes=[TOT[1]])
            totb = TOT[0].unsqueeze(2).to_broadcast([128, 4 * NCB, 64])
            yield
            if dirn == 1:
                S.op('dve', I('scalar_tensor_tensor', out=CUM[0], in0=CUM[0], scalar=-1.0, in1=LW[0], op0=ALU.mult, op1=ALU.add),
                     reads=[CUM[1], LW[1]], writes=[CUM[1]])
                yield
                S.op('dve', I('tensor_tensor', out=vc(CUM[0]), in0=vc(CUM[0]), in1=totb, op=ALU.add), reads=[CUM[1], TOT[1]], writes=[CUM[1]])
                yield
            S.op('pool', I('tensor_tensor', out=EXC[0], in0=CUM[0], in1=LW[0], op=ALU.subtract), reads=[CUM[1], LW[1]], writes=[EXC[1]])
            S.op('act', I('activation', out=EW[0], in_=TOT[0], func=AF.Exp), reads=[TOT[1]], writes=[EW[1]])
            S.op('act', I('activation', out=E1[0], in_=CUM[0], func=AF.Exp), reads=[CUM[1]], writes=[E1[1]])
            S.op('act', I('activation', out=E2[0], in_=CUM[0], func=AF.Exp, scale=-1.0), reads=[CUM[1]], writes=[E2[1]])
            yield
            S.op('dve', I('tensor_tensor', out=vc(DC[0]), in0=totb, in1=vc(CUM[0]), op=ALU.subtract), reads=[CUM[1], TOT[1], EXC[1]], writes=[DC[1]])
            S.op('act', I('activation', out=EXC[0], in_=EXC[0], func=AF.Exp), reads=[EXC[1]], writes=[EXC[1]])
            yield
            S.op('act', I('activation', out=DC[0], in_=DC[0], func=AF.Exp), reads=[DC[1]], writes=[DC[1]])
            AR5 = ARt[0].rearrange("p (h c s) -> p h c s", h=4, c=NCB)
            BK5 = BKt[0].rearrange("p (h c s) -> p h c s", h=4, c=NCB)
            S.op('dve', I('scalar_tensor_tensor', out=AR5[:, :, :, 0:64], in0=v4(KK[0]), scalar=-1.0, in1=v4(EXC[0]), op0=ALU.mult, op1=ALU.mult),
                 reads=[KK[1], EXC[1]], writes=[ARt[1]])
            S.op('pool', I('tensor_tensor', out=AR5[:, :, :, 64:128], in0=v4(Rf[0]), in1=v4(E1[0]), op=ALU.mult),
                 reads=[Rf[1], E1[1]], writes=[ARt[1]])
            yield
            S.op('dve', I('tensor_tensor', out=BK5[:, :, :, 0:64], in0=v4(Bv[0]), in1=v4(E2[0]), op=ALU.mult), reads=[Bv[1], E2[1]], writes=[BKt[1]])
            S.op('pool', I('tensor_tensor', out=BK5[:, :, :, 64:128], in0=v4(KD[0]), in1=v4(E2[0]), op=ALU.mult), reads=[KD[1], E2[1]], writes=[BKt[1]])
            yield
            S.op('dve', I('tensor_tensor', out=E1[0], in0=Bv[0], in1=DC[0], op=ALU.mult), reads=[Bv[1], DC[1], ARt[1]], writes=[E1[1]])
            yield
            to_tm(E1, BHt)
            yield
            S.op('dve', I('tensor_tensor', out=E2[0], in0=KD[0], in1=DC[0], op=ALU.mult), reads=[KD[1], DC[1], BKt[1]], writes=[E2[1]])
            yield
            to_tm(E2, KHt)
            yield
            to_tm(Vf, VTt)
            yield
            VT4 = v4(VTt[0]); BH4 = v4(BHt[0]); KH4 = v4(KHt[0])
            Hb3 = hv(Hb[0]); Hf3 = hv(Hf[0]); Ht3 = hv(Ht[0])
            EW3 = EW[0].rearrange("p (h c) -> p h c", h=4)
            chunks = list(range(NCB)) if dirn == 0 else list(range(NCB - 1, -1, -1))
            st = {}
            for c in chunks:
                ps, pb = psrot.next()
                p3 = ps[:, 0:512].rearrange("p (h t) -> p h t", h=4)
                headmm(lambda hp_, hs: p3[hsl(hs), hp_, :], lambda hp_, hs: BK5[hsl(hs), hp_, c, 0:64], lambda hp_, hs: AR5[hsl(hs), hp_, c, :],
                       [ARt[1], BKt[1]], [pb])
                scb, scbb = B['SCB'][c]
                S.op('dve', I('tensor_tensor', out=scb, in0=ps[:, 0:512], in1=mask4[:, dirn * 512:(dirn + 1) * 512], op=ALU.mult),
                     reads=[pb, cb], writes=[scbb])
                ps, pb = psrot.next()
                p3 = ps[:, 0:512].rearrange("p (h t) -> p h t", h=4)
                headmm(lambda hp_, hs: p3[hsl(hs), hp_, :], lambda hp_, hs: BK5[hsl(hs), hp_, c, 64:128], lambda hp_, hs: AR5[hsl(hs), hp_, c, :],
                       [ARt[1], BKt[1]], [pb])
                sck, sckb = B['SCK'][c]
                S.op('dve', I('tensor_tensor', out=sck, in0=ps[:, 0:512], in1=mask4[:, dirn * 512:(dirn + 1) * 512], op=ALU.mult),
                     reads=[pb, cb], writes=[sckb])
                ps, pb = psrot.next()
                p3 = ps[:, 0:256].rearrange("p (h t) -> p h t", h=4)
                headmm(lambda hp_, hs: p3[hsl(hs), hp_, :], lambda hp_, hs: AR5[hsl(hs), hp_, c, 0:64], lambda hp_, hs: BK5[hsl(hs), hp_, c, 0:64],
                       [ARt[1], BKt[1]], [pb])
                P, Pb_ = B['PP'][c].next()
                S.op('dve', I('tensor_tensor', out=P, in0=ps[:, 0:256], in1=maskt4[:, dirn * 256:(dirn + 1) * 256], op=ALU.mult),
                     reads=[pb, cb], writes=[Pb_])
                Pt, Ptb = B['PT'][c].next()
                S.op('pool', I('tensor_copy', out=hv(Pt), in_=hv512(scb)[:, :, 0:64]), reads=[scbb], writes=[Ptb])
                TT, TTb = B['TT'][c].next()
                S.op('pool', I('tensor_tensor', out=TT, in0=Pt, in1=id64B, op=ALU.add), reads=[Ptb, cb], writes=[TTb])
                st[c] = [P, Pb_, Pt, Ptb, TT, TTb]
                yield
            for k in range(1, 6):
                new = {}
                for c in chunks:
                    P, Pb_, Pt, Ptb, TT, TTb = st[c]
                    P3 = hv(P); Pt3 = hv(Pt)
                    ps, pb = psrot.next()
                    p3 = hv(ps[:, 0:256])
                    headmm(lambda hp_, hs: p3[hsl(hs), hp_, :], lambda hp_, hs: Pt3[hsl(hs), hp_, :], lambda hp_, hs: P3[hsl(hs), hp_, :],
                           [Pb_, Ptb], [pb])
                    Pn, Pnb = B['PP'][c].next()
                    evac_copy('act', Pn, ps[:, 0:256], [pb], [Pnb])
                    Ptn, Ptnb = Pt, Ptb
                    if k < 5:
                        ps2, pb2 = psrot.next()
                        p32 = hv(ps2[:, 0:256])
                        headmm(lambda hp_, hs: p32[hsl(hs), hp_, :], lambda hp_, hs: P3[hsl(hs), hp_, :], lambda hp_, hs: Pt3[hsl(hs), hp_, :],
                               [Pb_, Ptb], [pb2])
                        Ptn, Ptnb = B['PT'][c].next()
                        evac_copy('dve', Ptn, ps2[:, 0:256], [pb2], [Ptnb])
                    new[c] = (Pn, Pnb, Ptn, Ptnb)
                    yield
                for c in chunks:
                    P, Pb_, Pt, Ptb, TT, TTb = st[c]
                    Pn, Pnb, Ptn, Ptnb = new[c]
                    Pn3 = hv(Pn); TT3 = hv(TT)
                    ps3, pb3 = psrot.next()
                    p33 = hv(ps3[:, 0:256])
                    headmm(lambda hp_, hs: p33[hsl(hs), hp_, :], lambda hp_, hs: Pn3[hsl(hs), hp_, :], lambda hp_, hs: TT3[hsl(hs), hp_, :],
                           [Pnb, TTb], [pb3])
                    TTn, TTnb = B['TT'][c].next()
                    S.op('dve', I('tensor_tensor', out=TTn, in0=ps3[:, 0:256], in1=TT, op=ALU.add), reads=[pb3, TTb], writes=[TTnb])
                    st[c] = [Pn, Pnb, Ptn, Ptnb, TTn, TTnb]
                    yield
            for c in chunks:
                TT, TTb = st[c][4], st[c][5]
                TT3 = hv(TT)
                scb, scbb = B['SCB'][c]; sck, sckb = B['SCK'][c]
                scb3 = hv512(scb); sck3 = hv512(sck)
                ps, pb = psrot.next()
                pz = hv(ps[:, 0:256])
                headmm(lambda hp_, hs: pz[hsl(hs), hp_, :], lambda hp_, hs: AR5[hsl(hs), hp_, c, 0:64], lambda hp_, hs: Hb3[hsl(hs), hp_, :],
                       [ARt[1], Hb[1]], [pb],
                       more=[(lambda hp_, hs: sck3[hsl(hs), hp_, 0:64], lambda hp_, hs: VT4[hsl(hs), hp_, c, :], [sckb, VTt[1]])])
                zb, zbb = B['Zb'].next()
                evac_copy('act', zb, ps[:, 0:256], [pb], [zbb])
                zb3 = hv(zb)
                S.op('pool', I('tensor_tensor', out=Ht3, in0=Hf3, in1=EW3[:, :, c:c + 1].to_broadcast([128, 4, 64]), op=ALU.mult),
                     reads=[Hf[1], EW[1]], writes=[Ht[1]])
                yield
                ps, pb = psrot.next()
                pu_ = hv(ps[:, 0:256])
                headmm(lambda hp_, hs: pu_[hsl(hs), hp_, :], lambda hp_, hs: TT3[hsl(hs), hp_, :], lambda hp_, hs: zb3[hsl(hs), hp_, :],
                       [TTb, zbb], [pb])
                ub, ubb = B['Ub'].next()
                evac_copy('dve', ub, ps[:, 0:256], [pb], [ubb])
                ub3 = hv(ub)
                yield
                psh, pbh = psrot.next()
                ph = hv(psh[:, 0:256])
                headmm(lambda hp_, hs: ph[hsl(hs), hp_, :], lambda hp_, hs: BH4[hsl(hs), hp_, c, :], lambda hp_, hs: ub3[hsl(hs), hp_, :],
                       [BHt[1], ubb], [pbh],
                       more=[(lambda hp_, hs: KH4[hsl(hs), hp_, c, :], lambda hp_, hs: VT4[hsl(hs), hp_, c, :], [KHt[1], VTt[1]])])
                ps, pb = psrot.next()
                py = hv(ps[:, 0:256])
                headmm(lambda hp_, hs: py[hsl(hs), hp_, :], lambda hp_, hs: Hb3[hsl(hs), hp_, :], lambda hp_, hs: AR5[hsl(hs), hp_, c, 64:128],
                       [ARt[1], Hb[1]], [pb],
                       more=[(lambda hp_, hs: ub3[hsl(hs), hp_, :], lambda hp_, hs: scb3[hsl(hs), hp_, 64:128], [ubb, scbb]),
                             (lambda hp_, hs: VT4[hsl(hs), hp_, c, :], lambda hp_, hs: sck3[hsl(hs), hp_, 64:128], [VTt[1], sckb])])
                tok_end = (t0 + (c + 1) * 64) if dirn == 0 else (t0 + c * 64)
                if tok_end % SEG == 0:
                    S.op('dve', I('tensor_tensor', out=Hf[0], in0=Ht[0], in1=psh[:, 0:256], op=ALU.add), reads=[Ht[1], pbh], writes=[Hf[1]])
                    S.op('dve', I('tensor_scalar', out=Hf[0], in0=Hf[0], scalar1=flag[:, 0:1], scalar2=None, op0=ALU.mult),
                         reads=[Hf[1], cb], writes=[Hf[1]])
                    S.op('act', I('activation', out=Hb[0], in_=Hf[0], func=AF.Copy), reads=[Hf[1]], writes=[Hb[1]])
                else:
                    S.op('dve', I('tensor_tensor', out=Hb[0], in0=Ht[0], in1=psh[:, 0:256], op=ALU.add), reads=[Ht[1], pbh], writes=[Hb[1]])
                    S.op('dve', I('tensor_tensor', out=Hf[0], in0=Ht[0], in1=psh[:, 0:256], op=ALU.add), reads=[Ht[1], pbh], writes=[Hf[1]])
                yc, ycb = B['Yc'].next()
                evac_copy('act', yc, ps[:, 0:256], [pb], [ycb])
                S.dma('sp', I('dma_start', out=yr[:, :, t0 + c * 64: t0 + (c + 1) * 64], in_=hv(yc)), ycb, reads=[ycb])
                yield

    def hv512(ap):
        return ap.rearrange("p (h t) -> p h t", h=4)

    def run_threads(gens):
        gens = list(gens)
        while gens:
            for g in list(gens):
                try:
                    next(g)
                except StopIteration:
                    gens.remove(g)

    Bd = [alloc_dir(), alloc_dir()]
    run_threads([rw_dir(0, Bd[0]), rw_dir(1, Bd[1])])
    S.barrier()

    ar.reset()
    TE_ = T
    E4 = 4 * TE_

    def e3(ap):
        return ap.rearrange("p (h t) -> p h t", h=4)

    def alloc_epi():
        B = {}
        for nm_ in ('Rf', 'Kf', 'Vf', 'Yt', 'Yp', 'Aa', 'TMPa', 'KD0', 'KD1', 'GG'):
            B[nm_] = (ar.f32(E4), Buf())
        B['ADf'] = (ar.f32(TE_), Buf()); B['GDf'] = (ar.f32(TE_), Buf())
        B['adb'] = (ar.bf16(TE_), Buf()); B['sgb'] = (ar.bf16(TE_), Buf())
        B['YO'] = (ar.bf16(E4), Buf())
        return B

    def lora_sig_e(dst, upB, inb, biasc, d):
        for hp_ in range(4):
            ps, pb = psrot.next()
            S.op('pe', I('matmul', ps[:, 0:TE_], lhsT=upB[d * 64:(d + 1) * 64, hp_ * 128:(hp_ + 1) * 128],
                         rhs=inb[0][d * 64:(d + 1) * 64, :], start=True, stop=True, tile_position=(d * 64, 0)),
                 reads=[inb[1], cb], writes=[pb])
            S.op('act', I('activation', out=e3(dst[0])[:, hp_, :], in_=ps[:, 0:TE_], func=AF.Sigmoid,
                          bias=biasc[:, d * 4 + hp_: d * 4 + hp_ + 1], scale=1.0), reads=[pb, cb], writes=[dst[1]])

    def kdir_e(dst, a_t, Kf, TMPa):
        for hp_ in range(4):
            S.op('dve', I('tensor_scalar', out=e3(TMPa[0])[:, hp_, :], in0=e3(a_t[0])[:, hp_, :],
                          scalar1=rwt[:, 4 + hp_:5 + hp_], scalar2=omka[:, hp_:hp_ + 1], op0=ALU.mult, op1=ALU.add),
                 reads=[a_t[1], cb], writes=[TMPa[1]])
        S.op('pool', I('tensor_tensor', out=dst[0], in0=Kf[0], in1=TMPa[0], op=ALU.mult), reads=[Kf[1], TMPa[1]], writes=[dst[1]])

    def rw_epi(tiles, B):
        Rf, Kf, Vf, Yt, Yp, Aa, TMPa, KD0, KD1, GG, ADf, GDf, adb, sgb_, YO = (B[k] for k in (
            'Rf', 'Kf', 'Vf', 'Yt', 'Yp', 'Aa', 'TMPa', 'KD0', 'KD1', 'GG', 'ADf', 'GDf', 'adb', 'sgb', 'YO'))
        yr0 = YRW[0].rearrange("m p t -> p m t")
        yr1 = YRW[1].rearrange("m p t -> p m t")
        for i in tiles:
            t0 = i * TE_
            S.dma('sp', I('dma_start', out=e3(Rf[0]), in_=urw_r[:, 0:4, t0:t0 + TE_]), Rf[1], writes=[Rf[1]])
            S.dma('sp', I('dma_start', out=e3(Kf[0]), in_=urw_r[:, 4:8, t0:t0 + TE_]), Kf[1], writes=[Kf[1]])
            S.dma('sp', I('dma_start', out=e3(Vf[0]), in_=urw_r[:, 8:12, t0:t0 + TE_]), Vf[1], writes=[Vf[1]])
            S.dma('sp', I('dma_start', out=ADf[0], in_=URW[13, :, t0:t0 + TE_]), ADf[1], writes=[ADf[1]])
            S.dma('sp', I('dma_start', out=GDf[0], in_=URW[14, :, t0:t0 + TE_]), GDf[1], writes=[GDf[1]])
            S.dma('sp', I('dma_start', out=e3(Yt[0]), in_=yr0[:, :, t0:t0 + TE_]), Yt[1], writes=[Yt[1]])
            S.dma('sp', I('dma_start', out=e3(Yp[0]), in_=yr1[:, :, t0:t0 + TE_]), Yp[1], writes=[Yp[1]])
            yield
            S.op('pool', I('tensor_copy', out=adb[0], in_=ADf[0]), reads=[ADf[1]], writes=[adb[1]])
            S.op('act', I('activation', out=sgb_[0], in_=GDf[0], func=AF.Sigmoid), reads=[GDf[1]], writes=[sgb_[1]])
            S.op('dve', I('tensor_tensor', out=Yt[0], in0=Yt[0], in1=Yp[0], op=ALU.add), reads=[Yt[1], Yp[1]], writes=[Yt[1]])
            yield
            lora_sig_e(Aa, aupB, adb, a0t, 0)
            yield
            kdir_e(KD0, Aa, Kf, TMPa)
            yield
            lora_sig_e(Aa, aupB, adb, a0t, 1)
            yield
            kdir_e(KD1, Aa, Kf, TMPa)
            yield
            for hp_ in range(4):
                ps, pb = psrot.next()
                S.op('pe', I('matmul', ps[:, 0:TE_], lhsT=blk64F, rhs=e3(Yt[0])[:, hp_, :], start=True, stop=True), reads=[Yt[1], cb], writes=[pb])
                S.op('dve', I('tensor_tensor', out=e3(Yp[0])[:, hp_, :], in0=e3(Yt[0])[:, hp_, :], in1=ps[:, 0:TE_], op=ALU.subtract),
                     reads=[Yt[1], pb], writes=[Yp[1]])
            yield
            S.op('act', I('activation', out=GG[0], in_=Yp[0], func=AF.Square), reads=[Yp[1]], writes=[GG[1]])
            S.op('pool', I('tensor_tensor', out=KD0[0], in0=KD0[0], in1=KD1[0], op=ALU.add), reads=[KD0[1], KD1[1]], writes=[KD0[1]])
            yield
            S.op('pool', I('tensor_tensor', out=KD0[0], in0=KD0[0], in1=Rf[0], op=ALU.mult), reads=[KD0[1], Rf[1]], writes=[KD0[1]])
            for hp_ in range(4):
                ps, pb = psrot.next()
                S.op('pe', I('matmul', ps[:, 0:TE_], lhsT=blk64F, rhs=e3(GG[0])[:, hp_, :], start=True, stop=True), reads=[GG[1], cb], writes=[pb])
                S.op('act', I('activation', out=e3(TMPa[0])[:, hp_, :], in_=ps[:, 0:TE_], func=AF.Ln, bias=epsgn[:, 0:1], scale=1.0),
                     reads=[pb, cb], writes=[TMPa[1]])
            yield
            S.op('act', I('activation', out=TMPa[0], in_=TMPa[0], func=AF.Exp, scale=-0.5), reads=[TMPa[1]], writes=[TMPa[1]])
            for hp_ in range(4):
                S.op('pool', I('tensor_scalar', out=e3(KD0[0])[:, hp_, :], in0=e3(KD0[0])[:, hp_, :], scalar1=rwt[:, 8 + hp_:9 + hp_],
                               scalar2=None, op0=ALU.mult), reads=[KD0[1], cb], writes=[KD0[1]])
            yield
            S.op('dve', I('tensor_tensor', out=Yp[0], in0=Yp[0], in1=TMPa[0], op=ALU.mult), reads=[Yp[1], TMPa[1]], writes=[Yp[1]])
            yield
            for hp_ in range(4):
                S.op('dve', I('tensor_scalar', out=e3(Yp[0])[:, hp_, :], in0=e3(Yp[0])[:, hp_, :], scalar1=rwt[:, 12 + hp_:13 + hp_],
                              scalar2=rwt[:, 16 + hp_:17 + hp_], op0=ALU.mult, op1=ALU.add), reads=[Yp[1], cb], writes=[Yp[1]])
            yield
            for hp_ in range(4):
                ps, pb = psrot.next()
                S.op('pe', I('matmul', ps[:, 0:TE_], lhsT=blkF, rhs=e3(KD0[0])[:, hp_, :], start=True, stop=True), reads=[KD0[1], cb], writes=[pb])
                S.op('dve', I('tensor_tensor', out=e3(GG[0])[:, hp_, :], in0=ps[:, 0:TE_], in1=e3(Vf[0])[:, hp_, :], op=ALU.mult),
                     reads=[pb, Vf[1]], writes=[GG[1]])
                ps2, pb2 = psrot.next()
                S.op('pe', I('matmul', ps2[:, 0:TE_], lhsT=gupB[:, hp_ * 128:(hp_ + 1) * 128], rhs=sgb_[0], start=True, stop=True),
                     reads=[sgb_[1], cb], writes=[pb2])
                S.op('pool', I('tensor_tensor', out=e3(GG[0])[:, hp_, :], in0=e3(GG[0])[:, hp_, :], in1=e3(Yp[0])[:, hp_, :], op=ALU.add),
                     reads=[GG[1], Yp[1]], writes=[GG[1]])
                S.op('dve', I('tensor_tensor', out=e3(YO[0])[:, hp_, :], in0=e3(GG[0])[:, hp_, :], in1=ps2[:, 0:TE_], op=ALU.mult),
                     reads=[GG[1], pb2], writes=[YO[1]])
                yield
            S.dma('sp', I('dma_start', out=YMIX.rearrange("m p t -> p m t")[:, 0:4, t0:t0 + TE_], in_=e3(YO[0])), YO[1], reads=[YO[1]])
            yield

    Be = [alloc_epi(), alloc_epi()]
    run_threads([rw_epi(list(range(0, NTL, 2)), Be[0]), rw_epi(list(range(1, NTL, 2)), Be[1])])
    S.barrier()

    if stop == 'B2':
        return finish()
    ar.reset()
    R = ffn_resources(4, 3)
    xT3 = ar.f32(8 * T).rearrange("p (k t) -> p k t", k=8)
    xbufs = [Buf() for _ in range(KC)]
    ym = [(ar.bf16(8 * T).rearrange("p (k t) -> p k t", k=8), Buf()) for _ in range(2)]
    wo = (ar.bf16(8 * D).rearrange("p (k c) -> p k c", k=8), Buf())
    fT = (ar.f32(8 * T).rearrange("p (k t) -> p k t", k=8), Buf())
    yout = [(ar.f32(SUB * D).rearrange("p (s d) -> p s d", s=SUB), Buf()) for _ in range(2)]
    S.dma('sp', I('dma_start', out=wo[0], in_=wout_bf.rearrange("(kc p) n -> p kc n", p=128)), wo[1], writes=[wo[1]])
    for i in range(NTL):
        t0 = i * T
        S.dma('sp', I('dma_start', out=xT3, in_=X1T.rearrange("k p t -> p k t")[:, :, t0:t0 + T]), xbufs[0], writes=xbufs)
        ymt, ymb = ym[i % 2]
        S.dma('sp', I('dma_start', out=ymt, in_=YMIX.rearrange("k p t -> p k t")[:, :, t0:t0 + T]), ymb, writes=[ymb])
        for n in range(KC):
            ps, pb = psrot.next()
            for kc in range(KC):
                S.op('pe', I('matmul', ps[:, 0:T], lhsT=wo[0][:, kc, n * 128:(n + 1) * 128], rhs=ymt[:, kc, :],
                                                                          start=(kc == 0), stop=(kc == KC - 1)), reads=[wo[1], ymb], writes=[pb])
            S.op('dve', I('tensor_tensor', out=xT3[:, n, :], in0=xT3[:, n, :], in1=ps[:, 0:T], op=ALU.add),
                 reads=[pb, xbufs[n]], writes=[xbufs[n]])
        ffn(xT3, xbufs, 2, 1, R)
        rmsnorm(xT3, xbufs, 3, fT[0], fT[1], R['sq'][0], R['sq'][1], R['rst'][0], R['rst'][1])
        yo_, yob = yout[i % 2]
        for s in range(SUB):
            for g in range(2):
                ps, pb = psrot.next()
                for k4 in range(4):
                    kc = g * 4 + k4
                    S.op('pe', I('transpose', out=ps[:, k4 * 128:(k4 + 1) * 128], in_=fT[0][:, kc, s * 128:(s + 1) * 128],
                                                                               identity=identF), reads=[fT[1], cb], writes=[pb])
                evac_copy('act' if g else 'dve', yo_[:, s, g * 512:(g + 1) * 512], ps[:, 0:512], [pb], [yob])
        S.dma('sp', I('dma_start', out=y[t0:t0 + T, :].rearrange("(s p) d -> p s d", p=128), in_=yo_), yob, reads=[yob])
    S.emit()
    S.close()
    return nc


def _rope_tabs(NT, SEG_POS):
    ROT = 16
    inv = (500000.0 ** (-np.arange(0, ROT, 2, dtype=np.float32) / ROT)).astype(np.float32)
    pos = (np.arange(NT) % SEG_POS).astype(np.float32)
    ang = pos[:, None] * inv[None, :]
    c = np.cos(ang).astype(np.float32)
    s = np.sin(ang).astype(np.float32)
    cos = np.ones((128, NT), np.float32)
    sin = np.zeros((128, NT), np.float32)
    for half in range(2):
        for i in range(8):
            cos[half * 64 + i] = c[:, i]
            cos[half * 64 + 8 + i] = c[:, i]
            sin[half * 64 + i] = -s[:, i]
            sin[half * 64 + 8 + i] = s[:, i]
    return cos, sin


def _consts(T):
    ident = np.eye(128, dtype=np.float32)
    blk = np.zeros((128, 128), np.float32)
    blk[:64, :64] = 1
    blk[64:, 64:] = 1
    perm = np.zeros((128, 128), np.float32)
    for half in range(2):
        for i in range(8):
            perm[half * 64 + i + 8, half * 64 + i] = 1
            perm[half * 64 + i, half * 64 + i + 8] = 1
    s = np.arange(128) % 64
    t = np.arange(64)
    mask = np.zeros((128, 2, 128), np.float32)
    mask[:, 0, 0:64] = (s[:, None] < t[None, :])
    mask[:, 0, 64:128] = (s[:, None] <= t[None, :])
    mask[:, 1, 0:64] = (s[:, None] > t[None, :])
    mask[:, 1, 64:128] = (s[:, None] >= t[None, :])
    maskt = np.zeros((128, 2, 64), np.float32)
    maskt[:, 0, :] = (s[:, None] > t[None, :])
    maskt[:, 1, :] = (s[:, None] < t[None, :])
    id64 = (s[:, None] == t[None, :]).astype(np.float32)
    scan = np.ones((128, 4 * T), np.float32)
    scan[:, ::64] = 0
    return dict(cident=ident, cblk=blk, cperm=perm, cmask=mask, cmaskt=maskt, cid64=id64, cscan=scan)


def _col(v, n):
    return np.ascontiguousarray(np.asarray(v, np.float32).reshape(n, 128).T)


def make_inputs(inp, xs, flags, segpos, NT, T):
    f = lambda k: np.asarray(inp[k], np.float32)[0]
    shared = {}
    shared['wgu1'] = np.ascontiguousarray(f('ffn1_w_gu'))
    shared['wd1'] = np.ascontiguousarray(f('ffn1_w_down'))
    shared['wgu2'] = np.ascontiguousarray(f('ffn2_w_gu'))
    shared['wd2'] = np.ascontiguousarray(f('ffn2_w_down'))
    shared['win'] = np.ascontiguousarray(f('w_in'))
    shared['wout'] = np.ascontiguousarray(f('w_out'))
    shared['gvec'] = np.ascontiguousarray(np.stack([_col(f('ffn1_norm'), 8), _col(f('mix_norm'), 8), _col(f('ffn2_norm'), 8),
                                                    _col(f('final_norm'), 8)], axis=1))
    cw = f('conv_w')
    shared['taps'] = np.ascontiguousarray(cw.reshape(3, 15, 128).transpose(2, 0, 1))
    shared['tapsrow'] = np.ascontiguousarray(np.broadcast_to(cw[None], (128, 3, 1920)))
    shared['w0c'] = np.ascontiguousarray(f('rw_w0').reshape(2, 4, 128).transpose(2, 0, 1))
    shared['a0c'] = np.ascontiguousarray(f('rw_a0').reshape(2, 4, 128).transpose(2, 0, 1))
    shared['wup'] = np.ascontiguousarray(f('rw_w_up').reshape(128, 512))
    shared['aup'] = np.ascontiguousarray(f('rw_a_up').reshape(128, 512))
    shared['gup'] = np.ascontiguousarray(f('rw_g_up'))
    shared['rwv'] = np.ascontiguousarray(np.stack([_col(f('rw_k_k'), 4), _col(f('rw_k_a'), 4), _col(f('rw_r_k').reshape(-1), 4),
                                                   _col(f('rw_ln_g'), 4), _col(f('rw_ln_b'), 4)], axis=1))
    gq = f('da_q_norm')
    gk = f('da_k_norm')
    shared['gqk'] = np.ascontiguousarray(np.stack([np.tile(gq, 2), np.tile(gk, 2)], axis=1))
    shared['gqkb'] = np.ascontiguousarray(np.broadcast_to(np.stack([gq, gk])[None], (128, 2, 64)))
    shared['lamb'] = np.ascontiguousarray(np.broadcast_to(np.stack([f('da_lq1'), f('da_lk1'), f('da_lq2'), f('da_lk2')])[None], (128, 4, 64)))
    shared['sublnb'] = np.ascontiguousarray(np.broadcast_to(f('da_subln')[None], (128, 128)))
    shared.update(_consts(T))
    maps = []
    for c, xc in enumerate(xs):
        m = dict(shared)
        m['x'] = np.ascontiguousarray(xc, dtype=np.float32)
        cos, sin = _rope_tabs(NT, segpos[c])
        m['costab'] = cos
        m['sintab'] = sin
        m['flag'] = np.full((128, 1), flags[c], np.float32)
        maps.append(m)
    return maps


_NC_CACHE = {}


def kernel(**inputs):
    NT, SEG, T = 8192, 2048, 512
    xp = np.asarray(inputs['x_prompt'], np.float32)
    xs_ = np.asarray(inputs['x_sample'], np.float32)
    xs = [xp[b] for b in range(4)] + [xs_[4 * c:4 * c + 4].reshape(NT, D) for c in range(4)]
    flags = [1.0] * 4 + [0.0] * 4
    segpos = [8192] * 4 + [2048] * 4
    maps = make_inputs(inputs, xs, flags, segpos, NT, T)
    key = (NT, SEG, T)
    if key not in _NC_CACHE:
        _NC_CACHE[key] = build(NT, SEG, T)
    nc = _NC_CACHE[key]
    res = run_bass_kernel_spmd(nc, maps, core_ids=list(range(8)))
    outs = [np.asarray(r['y'], np.float32) for r in res.results]
    y_prompt = np.stack(outs[0:4], axis=0)
    y_sample = np.concatenate([o.reshape(4, 2048, D) for o in outs[4:8]], axis=0)
    return (y_prompt, y_sample)
```
